# Optimizing a Trainium2 kernel written in Bass

```python
import jax, jax.numpy as jnp
from jax import lax
import numpy as np

D_MODEL = 1024
BATCH = 32
SEQ = 2048
DEPTH = 2

DN_HEAD_DIM = 128
DN_HEADS = D_MODEL // (2 * DN_HEAD_DIM)
DN_WIDTH = DN_HEADS * DN_HEAD_DIM
DN_CONV = 5
DN_CHUNK = 64
HG_EXPAND = 128
HG_HEAD_DIM = 128
HG_HEADS = D_MODEL // (2 * HG_HEAD_DIM)
HG_KEY_WIDTH = HG_HEADS * HG_EXPAND
HG_VAL_WIDTH = HG_HEADS * HG_HEAD_DIM
HG_CHUNK = 32
D_FF = 2816
FFN_CONV = 3
EPS = 1e-6

IN_SIZES = (3 * DN_WIDTH, DN_WIDTH, 2 * DN_HEADS, 2 * DN_HEADS,
            HG_KEY_WIDTH, HG_KEY_WIDTH, HG_KEY_WIDTH, HG_VAL_WIDTH, HG_VAL_WIDTH,
            D_MODEL, D_MODEL)
IN_COLS = 4 * DN_WIDTH + 4 * DN_HEADS + 3 * HG_KEY_WIDTH + 2 * HG_VAL_WIDTH + 2 * D_MODEL

kernel_name = 'hybrid_deltanet_hgrn2_encoder'


def _split(x, sizes):
    idx = np.cumsum(np.array(sizes))[:-1].tolist()
    return jnp.split(x, idx, axis=-1)


def _rmsnorm(x, w):
    xf = x.astype(jnp.float32)
    y = xf * lax.rsqrt(jnp.mean(xf * xf, axis=-1, keepdims=True) + EPS)
    return (y * w.astype(jnp.float32)).astype(x.dtype)


def _gated_rmsnorm(o, z, w):
    y = o * lax.rsqrt(jnp.mean(o * o, axis=-1, keepdims=True) + EPS)
    return y * w.astype(jnp.float32) * jax.nn.silu(z)


def _l2norm(x):
    return x * lax.rsqrt(jnp.sum(x * x, axis=-1, keepdims=True) + EPS)


def _dwconv(x, w):
    return lax.conv_general_dilated(x, w[:, None, :].astype(x.dtype), window_strides=(1,),
                                    padding='SAME', dimension_numbers=('NWC', 'WIO', 'NWC'),
                                    feature_group_count=x.shape[-1])


def _to_heads(x, n):
    b, s, _ = x.shape
    return x.reshape(b, s, n, -1).transpose(0, 2, 1, 3)


def _gated_delta_rule(q, k, v, beta, g):
    b, h, s, dk = q.shape
    dv = v.shape[-1]
    c = DN_CHUNK
    n = s // c
    q = q.reshape(b, h, n, c, dk)
    k = k.reshape(b, h, n, c, dk)
    v = v.reshape(b, h, n, c, dv)
    beta = beta.reshape(b, h, n, c)
    gc = jnp.cumsum(g.reshape(b, h, n, c), axis=-1)
    incl = jnp.tril(jnp.ones((c, c), dtype=bool))
    strict = jnp.tril(jnp.ones((c, c), dtype=bool), k=-1)
    decay = jnp.exp(jnp.where(incl, gc[..., :, None] - gc[..., None, :], -jnp.inf))
    kb = k * beta[..., None]
    a = jnp.where(strict, jnp.einsum('bhntd,bhnsd->bhnts', kb, k) * decay, 0.0)
    eye = jnp.eye(c, dtype=q.dtype)
    t_inv = lax.linalg.triangular_solve(a + eye, jnp.broadcast_to(eye, a.shape),
                                        left_side=True, lower=True, unit_diagonal=True)
    u = jnp.einsum('bhnts,bhnsv->bhntv', t_inv, v * beta[..., None])
    w = jnp.einsum('bhnts,bhnsd->bhntd', t_inv, kb * jnp.exp(gc)[..., None])
    qk = jnp.einsum('bhntd,bhnsd->bhnts', q, k) * decay
    qg = q * jnp.exp(gc)[..., None]
    kd = k * jnp.exp(gc[..., -1:] - gc)[..., None]
    gl = jnp.exp(gc[..., -1])
    xs = (jnp.moveaxis(u, 2, 0), jnp.moveaxis(w, 2, 0), jnp.moveaxis(qk, 2, 0),
          jnp.moveaxis(qg, 2, 0), jnp.moveaxis(kd, 2, 0), jnp.moveaxis(gl, 2, 0))

    def step(state, inp):
        u_n, w_n, qk_n, qg_n, kd_n, gl_n = inp
        v_new = u_n - jnp.einsum('bhtd,bhdv->bhtv', w_n, state)
        o = jnp.einsum('bhtd,bhdv->bhtv', qg_n, state) + jnp.einsum('bhts,bhsv->bhtv', qk_n, v_new)
        state = state * gl_n[..., None, None] + jnp.einsum('bhsd,bhsv->bhdv', kd_n, v_new)
        return state, o

    state0 = jnp.zeros((b, h, dk, dv), dtype=q.dtype)
    _, o = lax.scan(step, state0, xs)
    return jnp.moveaxis(o, 0, 2).reshape(b, h, s, dv)


def _hgrn2_scan(q, k, i, logf):
    b, h, s, dk = q.shape
    dv = i.shape[-1]
    c = HG_CHUNK
    n = s // c
    q = jnp.moveaxis(q.reshape(b, h, n, c, dk), 2, 0)
    k = jnp.moveaxis(k.reshape(b, h, n, c, dk), 2, 0)
    i = jnp.moveaxis(i.reshape(b, h, n, c, dv), 2, 0)
    bc = jnp.moveaxis(jnp.cumsum(logf.reshape(b, h, n, c, dk), axis=-2), 2, 0)
    incl = jnp.tril(jnp.ones((c, c), dtype=bool))[:, :, None]

    def step(state, inp):
        q_n, k_n, i_n, b_n = inp
        dec = jnp.exp(jnp.where(incl, b_n[:, :, :, None, :] - b_n[:, :, None, :, :], -jnp.inf))
        att = jnp.einsum('bhtd,bhsd,bhtsd->bhts', q_n, k_n, dec)
        o = (jnp.einsum('bhtd,bhdv->bhtv', q_n * jnp.exp(b_n), state)
             + jnp.einsum('bhts,bhsv->bhtv', att, i_n))
        last = b_n[:, :, -1:, :]
        state = (state * jnp.exp(last[:, :, 0, :])[..., None]
                 + jnp.einsum('bhsd,bhsv->bhdv', k_n * jnp.exp(last - b_n), i_n))
        return state, o

    state0 = jnp.zeros((b, h, dk, dv), dtype=q.dtype)
    _, o = lax.scan(step, state0, (q, k, i, bc))
    return jnp.moveaxis(o, 0, 2).reshape(b, h, s, dv)


def _flip(t):
    return jnp.flip(t, axis=2)


def _mixer(h, w_in, dn_conv, dn_a_log, dn_dt_bias, dn_norm, lb, hg_norm, w_branch_dn, w_branch_hg, w_out):
    b, s, _ = h.shape
    f32 = jnp.float32
    p = h @ w_in
    (qkv, z, beta_r, a_r, hq, hf_fwd, hf_bwd, hi, hgate, gate_dn, gate_hg) = _split(p, IN_SIZES)

    qkv = jax.nn.silu(_dwconv(qkv, dn_conv)).astype(f32)
    q, k, v = _split(qkv, (DN_WIDTH, DN_WIDTH, DN_WIDTH))
    q = _l2norm(_to_heads(q, DN_HEADS)) * (DN_HEAD_DIM ** -0.5)
    k = _l2norm(_to_heads(k, DN_HEADS))
    v = _to_heads(v, DN_HEADS)
    beta = jax.nn.sigmoid(beta_r.astype(f32)).reshape(b, s, 2, DN_HEADS).transpose(2, 0, 3, 1)
    g = (-jnp.exp(dn_a_log.astype(f32))
         * jax.nn.softplus(a_r.astype(f32).reshape(b, s, 2, DN_HEADS) + dn_dt_bias.astype(f32)))
    g = g.transpose(2, 0, 3, 1)
    o_f = _gated_delta_rule(q, k, v, beta[0], g[0])
    o_b = _flip(_gated_delta_rule(_flip(q), _flip(k), _flip(v), _flip(beta[1]), _flip(g[1])))
    o_dn = (o_f + o_b).transpose(0, 2, 1, 3)
    o_dn = _gated_rmsnorm(o_dn, z.astype(f32).reshape(b, s, DN_HEADS, DN_HEAD_DIM), dn_norm)
    o_dn = o_dn.reshape(b, s, DN_WIDTH).astype(h.dtype)

    lb = lb.astype(f32)
    qh = _to_heads(jax.nn.silu(hq.astype(f32)), HG_HEADS) * (HG_EXPAND ** -0.5)
    ih = _to_heads(hi.astype(f32), HG_HEADS)
    outs = []
    for d, fr in enumerate((hf_fwd, hf_bwd)):
        fr = fr.astype(f32)
        logf = jax.nn.log_sigmoid(fr) + jnp.log1p(lb[d] * jnp.exp(-fr))
        kk = (1.0 - lb[d]) * jax.nn.sigmoid(-fr)
        outs.append((_to_heads(kk, HG_HEADS), _to_heads(logf, HG_HEADS)))
    o_hf = _hgrn2_scan(qh, outs[0][0], ih, outs[0][1])
    o_hb = _flip(_hgrn2_scan(_flip(qh), _flip(outs[1][0]), _flip(ih), _flip(outs[1][1])))
    o_hg = (o_hf + o_hb).transpose(0, 2, 1, 3)
    o_hg = _gated_rmsnorm(o_hg, hgate.astype(f32).reshape(b, s, HG_HEADS, HG_HEAD_DIM), hg_norm)
    o_hg = o_hg.reshape(b, s, HG_VAL_WIDTH).astype(h.dtype)

    merged = (jax.nn.sigmoid(gate_dn) * (o_dn @ w_branch_dn)
              + jax.nn.sigmoid(gate_hg) * (o_hg @ w_branch_hg))
    return merged @ w_out


def _conv_ffn(h, w_up, ffn_conv, ffn_conv_bias, w_down):
    gate, up = _split(h @ w_up, (D_FF, D_FF))
    gate = _dwconv(gate, ffn_conv) + ffn_conv_bias
    return (jax.nn.silu(gate) * up) @ w_down


def _log(v):
    return float(np.log(v))


def setup_inputs(seed: int = 0) -> dict:
    key = jax.random.key(seed)
    ks = jax.random.split(key, 20)
    f32 = jnp.float32
    nrm = lambda k, shape, scale: jax.random.normal(k, shape, f32) * scale
    dt = jnp.exp(jax.random.uniform(ks[5], (DEPTH, 2, DN_HEADS), f32)
                 * (_log(0.1) - _log(0.001)) + _log(0.001))
    return {
        'x': nrm(ks[0], (BATCH, SEQ, D_MODEL), 1.0),
        'mix_norm': 1.0 + nrm(ks[1], (DEPTH, D_MODEL), 0.02),
        'w_in': nrm(ks[2], (DEPTH, D_MODEL, IN_COLS), D_MODEL ** -0.5),
        'dn_conv': nrm(ks[3], (DEPTH, DN_CONV, 3 * DN_WIDTH), DN_CONV ** -0.5),
        'dn_a_log': jnp.log(jax.random.uniform(ks[4], (DEPTH, 2, DN_HEADS), f32, 1.0, 16.0)),
        'dn_dt_bias': dt + jnp.log(-jnp.expm1(-dt)),
        'dn_norm': 1.0 + nrm(ks[6], (DEPTH, DN_HEAD_DIM), 0.02),
        'hg_lb_logits': nrm(ks[7], (DEPTH, 2, HG_KEY_WIDTH), 1.0),
        'hg_norm': 1.0 + nrm(ks[8], (DEPTH, HG_HEAD_DIM), 0.02),
        'w_branch_dn': nrm(ks[9], (DEPTH, DN_WIDTH, D_MODEL), DN_WIDTH ** -0.5),
        'w_branch_hg': nrm(ks[10], (DEPTH, HG_VAL_WIDTH, D_MODEL), HG_VAL_WIDTH ** -0.5),
        'w_out': nrm(ks[11], (DEPTH, D_MODEL, D_MODEL), D_MODEL ** -0.5),
        'ffn_norm': 1.0 + nrm(ks[12], (DEPTH, D_MODEL), 0.02),
        'w_up': nrm(ks[13], (DEPTH, D_MODEL, 2 * D_FF), D_MODEL ** -0.5),
        'ffn_conv': nrm(ks[14], (DEPTH, FFN_CONV, D_FF), FFN_CONV ** -0.5),
        'ffn_conv_bias': nrm(ks[15], (DEPTH, D_FF), 0.02),
        'w_down': nrm(ks[16], (DEPTH, D_FF, D_MODEL), D_FF ** -0.5),
        'final_norm': 1.0 + nrm(ks[17], (D_MODEL,), 0.02),
    }


def reference(x, mix_norm, w_in, dn_conv, dn_a_log, dn_dt_bias, dn_norm, hg_lb_logits, hg_norm,
              w_branch_dn, w_branch_hg, w_out, ffn_norm, w_up, ffn_conv, ffn_conv_bias, w_down, final_norm):
    sm = jax.nn.softmax(hg_lb_logits.astype(jnp.float32), axis=0)
    lb_all = jnp.clip(jnp.cumsum(sm, axis=0) - sm[0:1], 0.0, 1.0)
    for l in range(DEPTH):
        h = _rmsnorm(x, mix_norm[l])
        x = x + _mixer(h, w_in[l], dn_conv[l], dn_a_log[l], dn_dt_bias[l], dn_norm[l], lb_all[l],
                       hg_norm[l], w_branch_dn[l], w_branch_hg[l], w_out[l])
        h = _rmsnorm(x, ffn_norm[l])
        x = x + _conv_ffn(h, w_up[l], ffn_conv[l], ffn_conv_bias[l], w_down[l])
    return _rmsnorm(x, final_norm)
```

```python
import numpy as np
import concourse.bass as bass
import concourse.mybir as mybir
from concourse.bass_utils import run_bass_kernel_spmd
from contextlib import ExitStack

F32 = mybir.dt.float32
BF16 = mybir.dt.bfloat16
AF = mybir.ActivationFunctionType
ALU = mybir.AluOpType

D = 1024
KT = 8
SEQ = 2048
PAD = 2
SP = SEQ + 2 * PAD
DEPTH = 2
DN_W = 512
INC = 6672
DFF = 2816
FT = 22
EPS = 1e-6
NCORES = 8
C_QKV = 0
C_Z = 1536
C_BETA = 2048
C_A = 2056
C_HQ = 2064
C_HF = (2576, 3088)
C_HI = 3600
C_HGATE = 4112
C_GDN = 4624
C_GHG = 5648

ENGS = ("pe", "act", "dve", "pool", "sp")
NDMA_SEM = 24
EPOCH = 4000


class Sched:
    def __init__(self, nc, stack, same_engine_sync=True):
        self.nc = nc
        self.stack = stack
        self.sem = {e: [] for e in ENGS}
        self.cnt = {e: 0 for e in ENGS}
        self.seen = {e: {} for e in ENGS}
        self.ops = {e: [] for e in ENGS}
        self.lastw = {}
        self.readers = {}
        self.dma_sems = [stack.enter_context(nc.semaphore("s_dma%d" % i)) for i in range(NDMA_SEM)]
        self.dma_uses = [0] * NDMA_SEM
        self.dma_rr = 0
        self.dma_rr_sw = 0
        self.same = same_engine_sync
        self.semobj = {}
        for i in range(NDMA_SEM):
            self.semobj[("d", i)] = self.dma_sems[i]

    def _deps(self, eng, reads, writes):
        need = {}

        def add(tok):
            if tok is None:
                return
            k, v = tok
            if k == ("e", eng) and (not self.same or eng == "pe"):
                return
            if need.get(k, 0) < v:
                need[k] = v
        for k in reads:
            add(self.lastw.get(k))
        for k in writes:
            add(self.lastw.get(k))
            for t in self.readers.get(k, {}).items():
                add(t)
        out = []
        for k, v in need.items():
            if self.seen[eng].get(k, 0) >= v:
                continue
            self.seen[eng][k] = v
            out.append((k, v))
        return out

    def _record(self, tok, reads, writes):
        for k in reads:
            d = self.readers.setdefault(k, {})
            if d.get(tok[0], 0) < tok[1]:
                d[tok[0]] = tok[1]
        for k in writes:
            self.lastw[k] = tok
            self.readers[k] = {}

    @staticmethod
    def _flat(keys):
        out = []
        for k in keys:
            if isinstance(k, (list, tuple)):
                out.extend(Sched._flat(k))
            else:
                out.append(k)
        return out

    def op(self, eng, name, reads=(), writes=(), **kw):
        reads, writes = self._flat(reads), self._flat(writes)
        fn = (name, kw)
        waits = self._deps(eng, reads, writes)
        self.cnt[eng] += 1
        tok = (("e", eng), self.cnt[eng])
        ep = (self.cnt[eng] - 1) // EPOCH
        while len(self.sem[eng]) <= ep:
            self.sem[eng].append(self.stack.enter_context(self.nc.semaphore("s_%s_%d" % (eng, len(self.sem[eng])))))
        self.ops[eng].append((waits, fn, (self.sem[eng][ep], 1)))
        self._record(tok, reads, writes)
        return tok

    def dma(self, eng, reads=(), writes=(), **kw):
        reads, writes = self._flat(reads), self._flat(writes)
        fn = ("dma_start", kw)
        half = NDMA_SEM // 2
        if eng == "pool":
            i = half + self.dma_rr_sw
            self.dma_rr_sw = (self.dma_rr_sw + 1) % half
        else:
            i = self.dma_rr
            self.dma_rr = (self.dma_rr + 1) % half
        waits = self._deps(eng, reads, writes)
        prev = self.dma_uses[i]
        k = ("d", i)
        if prev > 0 and self.seen[eng].get(k, 0) < 16 * prev:
            self.seen[eng][k] = 16 * prev
            waits.append((k, 16 * prev))
        self.dma_uses[i] += 1
        tok = (k, 16 * self.dma_uses[i])
        self.ops[eng].append((waits, fn, (self.dma_sems[i], 16)))
        self._record(tok, reads, writes)
        return tok

    def emit(self):
        nc = self.nc
        toks = []
        for e in ENGS:
            if self.cnt[e]:
                toks.append((("e", e), self.cnt[e]))
        for i in range(NDMA_SEM):
            if self.dma_uses[i]:
                toks.append((("d", i), 16 * self.dma_uses[i]))
        self.ops["sp"].append((toks, None, None))
        with nc.Block() as block:
            def replay(name):
                def body(eng):
                    for waits, fn, inc in self.ops[name]:
                        for k, v in waits:
                            if k[0] == "e":
                                ep = (v - 1) // EPOCH
                                eng.wait_ge(self.sem[k[1]][ep], v - ep * EPOCH)
                            else:
                                eng.wait_ge(self.semobj[k], v)
                        if fn is not None:
                            getattr(eng, fn[0])(**fn[1]).then_inc(inc[0], inc[1])
                return body
            block.tensor(replay("pe"))
            block.scalar(replay("act"))
            block.vector(replay("dve"))
            block.gpsimd(replay("pool"))
            block.sync(replay("sp"))


PC = {}
_off = 0
for l in range(DEPTH):
    for nm, n in (("mixn", 8), ("ffnn", 8), ("dnconv", 60), ("ffnconv", 66), ("ffnbias", 22),
                  ("dnnorm", 1), ("hgnorm", 1), ("alog", 8), ("dtb", 8)):
        PC[(nm, l)] = _off
        _off += n
PC["finaln"] = _off
_off += 8
PC["lblog"] = _off
_off += 16
NPC = _off


def pack_params(mix_norm, dn_conv, dn_a_log, dn_dt_bias, dn_norm, hg_lb_logits, hg_norm, ffn_norm, ffn_conv,
                ffn_conv_bias, final_norm):
    P = np.zeros((128, NPC), np.float32)
    for l in range(DEPTH):
        P[:, PC[("mixn", l)]:PC[("mixn", l)] + 8] = mix_norm[l].reshape(8, 128).T
        P[:, PC[("ffnn", l)]:PC[("ffnn", l)] + 8] = ffn_norm[l].reshape(8, 128).T
        P[:, PC[("dnconv", l)]:PC[("dnconv", l)] + 60] = dn_conv[l].reshape(5, 12, 128).transpose(2, 1, 0).reshape(128, 60)
        P[:, PC[("ffnconv", l)]:PC[("ffnconv", l)] + 66] = ffn_conv[l].reshape(3, 22, 128).transpose(2, 1, 0).reshape(128, 66)
        P[:, PC[("ffnbias", l)]:PC[("ffnbias", l)] + 22] = ffn_conv_bias[l].reshape(22, 128).T
        P[:, PC[("dnnorm", l)]] = dn_norm[l]
        P[:, PC[("hgnorm", l)]] = hg_norm[l]
        P[:, PC[("alog", l)]:PC[("alog", l)] + 8] = np.broadcast_to(dn_a_log[l].reshape(1, 8), (128, 8))
        P[:, PC[("dtb", l)]:PC[("dtb", l)] + 8] = np.broadcast_to(dn_dt_bias[l].reshape(1, 8), (128, 8))
    P[:, PC["finaln"]:PC["finaln"] + 8] = final_norm.reshape(8, 128).T
    P[:, PC["lblog"]:PC["lblog"] + 16] = hg_lb_logits.reshape(2, 2, 4, 128).transpose(3, 0, 1, 2).reshape(128, 16)
    return P


def pieces(total, maxn=512):
    n = (total + maxn - 1) // maxn
    base = total // n
    rem = total - base * n
    out = []
    a = 0
    for i in range(n):
        b = a + base + (1 if i < rem else 0)
        out.append((a, b))
        a = b
    return out


class Builder:
    def __init__(self, nseq, depth=DEPTH, do_mixer=True, do_ffn=True, do_dn=True, do_hg=True, debug=None, hg_stop=99, dbg_head=None):
        self.nseq = nseq
        self.depth = depth
        self.do_mixer = do_mixer
        self.do_ffn = do_ffn
        self.do_dn = do_dn
        self.do_hg = do_hg
        self.hg_stop = hg_stop
        self.dbg_head = dbg_head
        self.debug = debug or {}
        self.nc = bass.Bass("TRN2", target_bir_lowering=False)

    def sb(self, name, shape, dt):
        return self.st.enter_context(self.nc.sbuf_tensor(name, shape, dt))

    def hkeys(self, a, b):
        lo = max(a - PAD, 0)
        hi = min(b - PAD, SEQ)
        if hi <= lo:
            return []
        return ["hT%d" % c for c in range(lo // 512, (hi - 1) // 512 + 1)]

    def build(self):
        nc = self.nc
        nseq = self.nseq
        with ExitStack() as st:
            self.st = st
            S = self.S = Sched(nc, st)
            dr = {}
            dr["x"] = nc.dram_tensor("x", [nseq, SEQ, D], F32, kind="ExternalInput").ap()
            dr["params"] = nc.dram_tensor("params", [128, NPC], F32, kind="ExternalInput").ap()
            dr["w_in"] = nc.dram_tensor("w_in", [DEPTH, D, INC], F32, kind="ExternalInput").ap()
            dr["w_bdn"] = nc.dram_tensor("w_branch_dn", [DEPTH, DN_W, D], F32, kind="ExternalInput").ap()
            dr["w_bhg"] = nc.dram_tensor("w_branch_hg", [DEPTH, DN_W, D], F32, kind="ExternalInput").ap()
            dr["w_out"] = nc.dram_tensor("w_out", [DEPTH, D, D], F32, kind="ExternalInput").ap()
            dr["w_up"] = nc.dram_tensor("w_up", [DEPTH, D, 2 * DFF], F32, kind="ExternalInput").ap()
            dr["w_down"] = nc.dram_tensor("w_down", [DEPTH, DFF, D], F32, kind="ExternalInput").ap()
            dr["out"] = nc.dram_tensor("out", [nseq, SEQ, D], F32, kind="ExternalOutput").ap()
            for k, shp in self.debug.items():
                dr[k] = nc.dram_tensor(k, list(shp), F32, kind="ExternalOutput").ap()
            self.dr = dr

            self.xT = self.sb("xT", [128, KT, SEQ], F32)
            self.hT = self.sb("hT", [128, KT, SP], BF16)
            self.prm = self.sb("prm", [128, NPC], F32)
            self.identf = self.sb("identf", [128, 128], F32)
            self.identb = self.sb("identb", [128, 128], BF16)
            self.onesb = self.sb("onesb", [128, 128], BF16)
            self.wslot = [self.sb("wslot%d" % i, [128, KT, 512], BF16) for i in range(2)]
            self.wrr = 0
            self.F8 = [self.sb("F8_%d" % i, [128, SP], F32) for i in range(4)]
            self.actT = self.sb("actT", [128, 8, SEQ], BF16)
            self.wd = self.sb("wd", [128, 8, D], BF16)
            self.rmask32 = self.sb("rmask32", [128, SEQ], mybir.dt.float8e4)
            self.mask8 = [self.sb("mask8_%d" % d, [64, 64], BF16) for d in range(2)]
            self.lbT = self.sb("lbT", [128, 2, 16], F32)
            self.Ebuf = self.sb("Ebuf", [128, 2, 64], F32)
            self.totb = self.sb("totb", [128, 64], F32)
            self.Sst = [self.sb("Sst%d" % d, [128, 128], F32) for d in range(2)]
            self.Sbf = [[self.sb("Sbf%d_%d" % (d, r), [128, 128], BF16) for r in range(3)] for d in range(2)]
            self.cf = {nm: self.sb("cf_" + nm, [128, 128], F32) for nm in ("U2", "L2", "B2", "H0", "H1", "ones", "M1f", "M1b", "M2f", "M2b")}
            self.mask16 = self.sb("mask16", [128, 128], BF16)
            self.dummy = self.sb("dummyb", [128, 2], F32)
            tkn = ("beta", "nbeta", "g", "gc", "ngc", "tot", "eg", "ekd")
            self.tk = {nm: self.F8[2][:, 1024 + i * 128:1024 + (i + 1) * 128].rearrange("p (t c) -> p t c", c=8) for i, nm in enumerate(tkn)}
            self.glb = self.F8[3][:, 1024:1280].rearrange("p (h t c) -> p h t c", h=2, t=16)
            self.mtf = {nm: self.F8[3][:, 1280 + i * 128:1280 + (i + 1) * 128] for i, nm in enumerate(("dg", "EA", "EQ"))}
            self.nea = self.F8[3][:, 1664:1672]
            mtv = self.F8[1][:, 0:1024].bitcast(BF16)
            self.mt = {nm: mtv[:, i * 256:(i + 1) * 256] for i, nm in enumerate(("B", "BT", "SQ", "SQT", "N", "Z", "Y0", "Y0b"))}
            self.dn_small_keys = ["tk", "glb", "nea", "mtf_dg", "mtf_EA", "mtf_EQ", "vn0", "vn1"] + ["mt_" + nm for nm in ("B", "BT", "SQ", "SQT", "N", "Z", "Y0", "Y0b")]
            dr["o_dn_s"] = nc.dram_tensor("o_dn_s", [4, 128, SEQ], BF16, kind="Internal").ap()
            dr["o_hg_s"] = nc.dram_tensor("o_hg_s", [4, 128, SEQ], BF16, kind="Internal").ap()
            self.banks = [st.enter_context(nc.psum_tensor("bank%d" % i, [128, 512], F32)) for i in range(8)]
            self.brr = 0
            self.rot = list(range(8))

            self.setup_consts()
            try:
                self.body()
            except StopIteration:
                pass
            S.emit()
        return nc

    def body(self):
        if True:
            nseq = self.nseq
            for s in range(nseq):
                self.load_x(s)
                for l in range(self.depth):
                    if self.do_mixer:
                        self.mixer(l)
                    if self.do_ffn:
                        self.ffn(l)
                self.final_store(s)

    def nextbank(self):
        self.brr = (self.brr + 1) % len(self.rot)
        return self.rot[self.brr]

    def setup_consts(self):
        S = self.S
        prm, dr = self.prm, self.dr
        S.dma("sp", writes=["prm"], out=prm[:], in_=dr["params"][:, :])
        identf, identb, onesb, hT = self.identf, self.identb, self.onesb, self.hT
        S.op("pool", "memset", writes=["identf"], ap=identf[:], constant=1.0)
        S.op("pool", "affine_select", reads=["identf"], writes=["identf"], out=identf[:], in_=identf[:], pattern=[[-1, 128]],
             compare_op=ALU.is_equal, fill=0.0, base=0, channel_multiplier=1)
        S.op("dve", "tensor_copy", reads=["identf"], writes=["identb"], out=identb[:], in_=identf[:])
        S.op("dve", "memset", writes=["onesb"], ap=onesb[:], constant=1.0)
        S.op("dve", "memset", writes=["hTpad"], ap=hT[:, :, 0:PAD], constant=0.0)
        S.op("dve", "memset", writes=["hTpad"], ap=hT[:, :, PAD + SEQ:SP], constant=0.0)
        rm = self.rmask32
        S.op("pool", "memset", writes=["rmask32"], ap=rm[:], constant=1.0)
        S.op("pool", "memset", reads=["rmask32"], writes=["rmask32"], ap=rm[:, 0:SEQ:32], constant=0.0)
        for d in range(2):
            mk = self.mask8[d]
            key = "mask8_%d" % d
            S.op("pool", "memset", writes=[key], ap=mk[:], constant=1.0)
            sg = 1 if d == 0 else -1
            S.op("pool", "affine_select", reads=[key], writes=[key], out=mk[:, :], in_=mk[:, :],
                 pattern=[[sg, 64]], compare_op=ALU.is_ge, fill=0.0, base=0, channel_multiplier=-sg)
            if d == 0:
                S.op("pool", "memset", reads=[key], writes=[key], ap=mk[0:32, 32:64], constant=0.0)
            else:
                S.op("pool", "memset", reads=[key], writes=[key], ap=mk[32:64, 0:32], constant=0.0)
        cf = self.cf
        BIG = 30000.0
        def tri(nm, sg, val_in, val_out, extra_zero=None, incl=True):
            t = cf[nm]
            S.op("pool", "memset", writes=["cf_" + nm], ap=t[:], constant=val_in)
            S.op("pool", "affine_select", reads=["cf_" + nm], writes=["cf_" + nm], out=t[:, :], in_=t[:, :], pattern=[[sg, 128]],
                 compare_op=ALU.is_ge, fill=val_out, base=(0 if incl else -1), channel_multiplier=-sg)
            S.op("pool", "memset", reads=["cf_" + nm], writes=["cf_" + nm], ap=t[0:64, 64:128], constant=val_out)
            S.op("pool", "memset", reads=["cf_" + nm], writes=["cf_" + nm], ap=t[64:128, 0:64], constant=val_out)
        tri("U2", 1, 1.0, 0.0)
        tri("L2", -1, 1.0, 0.0)
        tri("M1f", -1, 0.0, BIG, incl=False)
        tri("M1b", 1, 0.0, BIG, incl=False)
        tri("M2f", 1, 0.0, -BIG)
        tri("M2b", -1, 0.0, -BIG)
        S.op("pool", "memset", writes=["cf_B2"], ap=cf["B2"][:], constant=1.0)
        S.op("pool", "memset", reads=["cf_B2"], writes=["cf_B2"], ap=cf["B2"][0:64, 64:128], constant=0.0)
        S.op("pool", "memset", reads=["cf_B2"], writes=["cf_B2"], ap=cf["B2"][64:128, 0:64], constant=0.0)
        S.op("pool", "memset", writes=["cf_ones"], ap=cf["ones"][:], constant=1.0)
        for nm, r0 in (("H0", 0), ("H1", 64)):
            S.op("pool", "memset", writes=["cf_" + nm], ap=cf[nm][:], constant=0.0)
            S.op("pool", "memset", reads=["cf_" + nm], writes=["cf_" + nm], ap=cf[nm][r0:r0 + 1, :], constant=1.0)
        m16 = self.mask16
        S.op("pool", "memset", writes=["mask16"], ap=m16[:], constant=0.0)
        for b in range(8):
            p0 = (b * 16) // 32 * 32
            pass
        S.op("pool", "memset", reads=["mask16"], writes=["mask16"], ap=m16[:], constant=1.0)
        for b in range(8):
            S.op("pool", "affine_select", reads=["mask16"], writes=["mask16"], out=m16[:, b * 16:(b + 1) * 16], in_=m16[:, b * 16:(b + 1) * 16],
                 pattern=[[0, 16]], compare_op=ALU.is_ge, fill=0.0, base=-16 * b, channel_multiplier=1)
            S.op("pool", "affine_select", reads=["mask16"], writes=["mask16"], out=m16[:, b * 16:(b + 1) * 16], in_=m16[:, b * 16:(b + 1) * 16],
                 pattern=[[0, 16]], compare_op=ALU.is_ge, fill=0.0, base=16 * b + 15, channel_multiplier=-1)
        lbT = self.lbT
        o = PC["lblog"]
        S.op("dve", "memset", writes=["lbT"], ap=lbT[:, 0, 0:8], constant=0.0)
        S.op("dve", "tensor_tensor", reads=["prm", "lbT"], writes=["lbT"], out=lbT[:, 0, 8:16], in0=prm[:, o + 8:o + 16], in1=prm[:, o:o + 8], op=ALU.subtract)
        S.op("act", "activation", reads=["lbT"], writes=["lbT"], out=lbT[:, 0, 8:16], in_=lbT[:, 0, 8:16], func=AF.Sigmoid)
        S.op("dve", "tensor_scalar", reads=["lbT"], writes=["lbT"], out=lbT[:, 1, :], in0=lbT[:, 0, :], scalar1=-1.0, scalar2=1.0, op0=ALU.mult, op1=ALU.add)

    def pcol(self, name, l=None, j=0, n=1):
        o = PC[(name, l)] if l is not None else PC[name]
        return self.prm[:, o + j:o + j + n]

    def evac(self, eng, out, in_, reads, writes):
        if eng == "act":
            self.S.op("act", "activation", reads=reads, writes=writes, out=out, in_=in_, func=AF.Identity)
        else:
            self.S.op(eng, "tensor_copy", reads=reads, writes=writes, out=out, in_=in_)

    def load_x(self, s):
        S, dr, xT, identf = self.S, self.dr, self.xT, self.identf
        for g in range(4):
            for i in range(4):
                tt = g * 4 + i
                S.dma("sp", writes=["F8_%d" % (i // 2)], out=self.F8[i // 2][:, (i % 2) * D:(i % 2 + 1) * D], in_=dr["x"][s, tt * 128:(tt + 1) * 128, :])
            for kt in range(KT):
                b = self.nextbank()
                bank = self.banks[b]
                for i in range(4):
                    S.op("pe", "transpose", reads=["F8_%d" % (i // 2), "identf"], writes=["bank%d" % b],
                         out=bank[:, i * 128:(i + 1) * 128], in_=self.F8[i // 2][:, (i % 2) * D + kt * 128:(i % 2) * D + (kt + 1) * 128], identity=identf[:])
                self.evac("act" if kt % 2 == 0 else "dve", xT[:, kt, g * 512:(g + 1) * 512], bank[:, :],
                          ["bank%d" % b], ["xT%d_%d" % (kt, g)])

    def rstd_chunk(self, c, dst, dstkey, sqbuf, sqkey):
        S, xT, onesb = self.S, self.xT, self.onesb
        b = self.nextbank()
        bank = self.banks[b]
        for kt in range(KT):
            S.op("act", "activation", reads=["xT%d_%d" % (kt, c)], writes=["actT%d_%d" % (kt, c)],
                 out=sqbuf[:, kt, :], in_=xT[:, kt, c * 512:(c + 1) * 512], func=AF.Square)
            S.op("pe", "matmul", reads=["onesb", "actT%d_%d" % (kt, c)], writes=["bank%d" % b],
                 out=bank[:, :], lhsT=onesb[:], rhs=sqbuf[:, kt, :], start=(kt == 0), stop=(kt == KT - 1))
        S.op("act", "activation", reads=["bank%d" % b], writes=[dstkey], out=dst, in_=bank[:, :], func=AF.Sqrt, bias=EPS, scale=1.0 / D)
        S.op("dve", "reciprocal", reads=[dstkey], writes=[dstkey], out=dst, in_=dst)

    def rmsnorm_to_hT(self, wname, l):
        S, xT, hT = self.S, self.xT, self.hT
        for c in range(4):
            rst = self.F8[3][:, (c % 2) * 512:(c % 2 + 1) * 512]
            sqv = self.actT[:, 0:8, c * 512:(c + 1) * 512]
            self.rstd_chunk(c, rst, "F8_3", sqv, "sq%d_" % c)
            for kt in range(KT):
                S.op("dve", "scalar_tensor_tensor", reads=["xT%d_%d" % (kt, c), "F8_3", "prm"], writes=["hT%d" % c],
                     out=hT[:, kt, PAD + c * 512:PAD + (c + 1) * 512], in0=xT[:, kt, c * 512:(c + 1) * 512],
                     scalar=self.pcol(wname, l, kt), in1=rst, op0=ALU.mult, op1=ALU.mult)

    def load_w(self, src_ap, ncols, nk=KT):
        i = self.wrr
        self.wrr = (self.wrr + 1) % len(self.wslot)
        slot = self.wslot[i]
        self.S.dma("pool", writes=[self.WK(i)], out=slot[:, 0:nk, 0:ncols], in_=src_ap.rearrange("(kt p) n -> p kt n", p=128))
        return slot, self.WK(i)

    def proj(self, slot, skey, col, a, b, bank_i, nk=KT, src=None, srckeys=None, ncol=128):
        bank = self.banks[bank_i]
        src = self.hT if src is None else src
        keys = (self.hkeys(a, b) + ["hTpad"]) if srckeys is None else srckeys
        for kt in range(nk):
            self.S.op("pe", "matmul", reads=[skey] + keys, writes=["bank%d" % bank_i],
                      out=bank[0:ncol, 0:b - a], lhsT=slot[:, kt, col:col + ncol], rhs=src[:, kt, a:b],
                      start=(kt == 0), stop=(kt == nk - 1))

    def ffn(self, l):
        S, dr, xT, actT, wd = self.S, self.dr, self.xT, self.actT, self.wd
        self.rmsnorm_to_hT("ffnn", l)
        gpre, cv = self.F8[0], self.F8[1]
        sl = cv
        for (h0, h1) in [(0, 8), (8, 16), (16, 22)]:
            nh = h1 - h0
            S.dma("pool", writes=["wd"], out=wd[:, 0:nh, :],
                  in_=dr["w_down"][l, h0 * 128:h1 * 128, :].rearrange("(kt p) n -> p kt n", p=128))
            for g0 in range(h0, h1, 4):
                g1 = min(g0 + 4, h1)
                ng = g1 - g0
                gslot, gkey = self.load_w(dr["w_up"][l, :, g0 * 128:g1 * 128], ng * 128)
                uslot, ukey = self.load_w(dr["w_up"][l, :, DFF + g0 * 128:DFF + g1 * 128], ng * 128)
                for f in range(g0, g1):
                    fi = f - h0
                    col = (f - g0) * 128
                    for (a, b) in pieces(SP):
                        bi = self.nextbank()
                        self.proj(gslot, gkey, col, a, b, bi)
                        self.evac("act", gpre[:, a:b], self.banks[bi][:, 0:b - a], ["bank%d" % bi], ["F8_0"])
                    S.op("dve", "tensor_scalar", reads=["F8_0", "prm"], writes=["F8_1"], out=cv[:, 0:SEQ], in0=gpre[:, 1:1 + SEQ],
                         scalar1=self.pcol("ffnconv", l, f * 3), scalar2=self.pcol("ffnbias", l, f), op0=ALU.mult, op1=ALU.add)
                    for k in (1, 2):
                        S.op("dve", "scalar_tensor_tensor", reads=["F8_0", "F8_1", "prm"], writes=["F8_1"], out=cv[:, 0:SEQ],
                             in0=gpre[:, 1 + k:1 + k + SEQ], scalar=self.pcol("ffnconv", l, f * 3 + k), in1=cv[:, 0:SEQ],
                             op0=ALU.mult, op1=ALU.add)
                    S.op("act", "activation", reads=["F8_1"], writes=["F8_1"], out=sl[:, 0:SEQ], in_=cv[:, 0:SEQ], func=AF.Silu)
                    for c in range(4):
                        bi = self.nextbank()
                        self.proj(uslot, ukey, col, PAD + c * 512, PAD + (c + 1) * 512, bi)
                        S.op("dve", "tensor_tensor", reads=["bank%d" % bi, "F8_1"], writes=["actT%d_%d" % (fi, c)],
                             out=actT[:, fi, c * 512:(c + 1) * 512], in0=self.banks[bi][:, :], in1=sl[:, c * 512:(c + 1) * 512], op=ALU.mult)
            for dm in range(KT):
                for c in range(4):
                    bi = self.nextbank()
                    bank = self.banks[bi]
                    for kt in range(nh):
                        S.op("pe", "matmul", reads=["wd", "actT%d_%d" % (kt, c)], writes=["bank%d" % bi], out=bank[:, :],
                             lhsT=wd[:, kt, dm * 128:(dm + 1) * 128], rhs=actT[:, kt, c * 512:(c + 1) * 512], start=(kt == 0), stop=(kt == nh - 1))
                    S.op("dve", "tensor_tensor", reads=["bank%d" % bi, "xT%d_%d" % (dm, c)], writes=["xT%d_%d" % (dm, c)],
                         out=xT[:, dm, c * 512:(c + 1) * 512], in0=bank[:, :], in1=xT[:, dm, c * 512:(c + 1) * 512], op=ALU.add)

    def WK(self, i):
        return ["wslot%d_%d" % (i, q) for q in range(4)]

    def AK(self, i):
        return ["actT%d_%d" % (i, c) for c in range(4)]

    def load_w_cols(self, wsrc, cols, width=128):
        i = self.wrr
        self.wrr = (self.wrr + 1) % len(self.wslot)
        slot = self.wslot[i]
        for q, c0 in enumerate(cols):
            self.S.dma("pool", writes=([self.WK(i)[q]] if width == 128 else [self.WK(i)]), out=slot[:, :, q * width:(q + 1) * width],
                       in_=wsrc[:, c0:c0 + width].rearrange("(kt p) n -> p kt n", p=128))
        return slot, self.WK(i)

    def proj_fm(self, slot, skey, col, dst, dkeys, func, scale=1.0):
        for c in range(4):
            bi = self.nextbank()
            self.proj(slot, skey, col, PAD + c * 512, PAD + (c + 1) * 512, bi)
            self.S.op("act", "activation", reads=["bank%d" % bi], writes=dkeys, out=dst[:, c * 512:(c + 1) * 512],
                      in_=self.banks[bi][:, :], func=func, scale=scale)

    def mixer(self, l):
        self.rmsnorm_to_hT("mixn", l)
        if self.do_dn:
            self.barrier()
            self.dn_scalars(l)
            for j in (range(4) if self.dbg_head is None else (self.dbg_head,)):
                self.dn_head(l, j)
            if self.dbg_head is not None:
                raise StopIteration
            self.barrier()
        if self.do_hg:
            for j in range(4):
                self.hg_head(l, j)
        self.merge(l)

    def barrier(self):
        keys = ["F8_1", "F8_2", "F8_3"] + self.dn_small_keys
        self.S.op("dve", "memset", reads=[], writes=keys + ["dummy"], ap=self.dummy[:], constant=0.0)

    def hg_head(self, l, j):
        S, dr, A, hT = self.S, self.dr, self.actT, self.hT
        F0, F1, F2, F3 = [f[:, 0:SEQ] for f in self.F8]
        wsrc = dr["w_in"][l]
        slot, skey = self.load_w_cols(wsrc, [C_HQ + j * 128, C_HF[0] + j * 128, C_HF[1] + j * 128, C_HI + j * 128])
        self.proj_fm(slot, skey, 0, F0, ["F8_0"], AF.Silu)
        def itok(m):
            return A[0:64, 6 + m // 16, (m % 16) * 128:(m % 16 + 1) * 128], "actT%d_%d" % (6 + m // 16, (m % 16) // 4)
        for g in range(8):
            bi = self.nextbank()
            bank = self.banks[bi]
            for q in range(4):
                m = g * 4 + q
                for kt in range(KT):
                    S.op("pe", "matmul", reads=[skey] + self.hkeys(PAD + m * 64, PAD + (m + 1) * 64), writes=["bank%d" % bi],
                         out=bank[0:64, q * 128:(q + 1) * 128], lhsT=hT[:, kt, PAD + m * 64:PAD + (m + 1) * 64],
                         rhs=slot[:, kt, 3 * 128:4 * 128], start=(kt == 0), stop=(kt == KT - 1))
            m0 = g * 4
            dst = A[0:64, 6 + m0 // 16, (m0 % 16) * 128:(m0 % 16) * 128 + 512]
            self.evac("act" if g % 2 == 0 else "dve", dst, bank[0:64, :], ["bank%d" % bi], ["actT%d_%d" % (6 + m0 // 16, (m0 % 16) // 4)])
        Eb, totb = self.Ebuf, self.totb
        if self.hg_stop <= 1:
            return
        for d in range(2):
            li = l * 8 + d * 4 + j
            lb = self.lbT[:, 0, li:li + 1]
            omlb = self.lbT[:, 1, li:li + 1]
            self.proj_fm(slot, skey, (1 + d) * 128, F1, ["F8_1"], AF.Sigmoid)
            S.op("dve", "tensor_scalar", reads=["F8_1", "lbT"], writes=["F8_1"], out=F1, in0=F1, scalar1=omlb, scalar2=lb, op0=ALU.mult, op1=ALU.add)
            S.op("dve", "tensor_scalar", reads=["F8_1"], writes=["F8_2"], out=F2, in0=F1, scalar1=-1.0, scalar2=1.0, op0=ALU.mult, op1=ALU.add)
            S.op("act", "activation", reads=["F8_1"], writes=["F8_1"], out=F1, in_=F1, func=AF.Ln)
            S.op("dve", "tensor_tensor_scan", reads=["F8_1", "rmask32"], writes=["F8_3"], out=F3, data0=self.rmask32[:], data1=F1,
                 initial=0.0, op0=ALU.mult, op1=ALU.add)
            if self.hg_stop <= 2:
                continue
            S.op("dve", "tensor_copy", reads=["F8_3"], writes=["totb"], out=totb[:], in_=self.F8[3][:, 31:SEQ:32])
            S.op("act", "activation", reads=["totb"], writes=["Ebuf%d" % d], out=Eb[:, d, :], in_=totb[:], func=AF.Exp)
            tot_bc = totb[:].unsqueeze(2).to_broadcast([128, 64, 32])
            v3 = lambda ap: ap.rearrange("p (c t) -> p c t", t=32)
            if d == 0:
                S.op("dve", "tensor_tensor", reads=["totb", "F8_3"], writes=["F8_1"], out=v3(F1), in0=tot_bc, in1=v3(F3), op=ALU.subtract)
                ysc = 1.0
            else:
                S.op("dve", "tensor_tensor", reads=["F8_1", "F8_3"], writes=["F8_1"], out=F1, in0=F1, in1=F3, op=ALU.subtract)
                S.op("dve", "tensor_tensor", reads=["totb", "F8_1"], writes=["F8_3"], out=v3(F3), in0=v3(F1), in1=tot_bc, op=ALU.add)
                ysc = -1.0
            S.op("act", "activation", reads=["F8_1"], writes=["F8_1"], out=F1, in_=F1, func=AF.Exp, scale=ysc)
            S.op("dve", "tensor_tensor", reads=["F8_1", "F8_2"], writes=self.AK(5), out=A[:, 5, :], in0=F2, in1=F1, op=ALU.mult)
            S.op("act", "activation", reads=["F8_3"], writes=["F8_1"], out=F1, in_=F3, func=AF.Exp, scale=-1.0)
            S.op("dve", "tensor_tensor", reads=["F8_1", "F8_2"], writes=self.AK(4), out=A[:, 4, :], in0=F2, in1=F1, op=ALU.mult)
            S.op("act", "activation", reads=["F8_3"], writes=["F8_3"], out=F3, in_=F3, func=AF.Exp)
            S.op("dve", "scalar_tensor_tensor", reads=["F8_0", "F8_3"], writes=self.AK(d), out=A[:, d, :], in0=F0, scalar=float(128 ** -0.5), in1=F3,
                 op0=ALU.mult, op1=ALU.mult)
            if self.hg_stop <= 3:
                continue
            for g in range(8):
                bi = self.nextbank()
                bank = self.banks[bi]
                for q in range(4):
                    m = g * 4 + q
                    S.op("pe", "matmul", reads=["actT5_%d" % (m // 8), "identb"], writes=["bank%d" % bi], out=bank[0:64, q * 128:(q + 1) * 128],
                         lhsT=A[:, 5, m * 64:(m + 1) * 64], rhs=self.identb[:], start=True, stop=True)
                dst = self.wd[0:64, 4 * d + g // 2, (g % 2) * 512:(g % 2 + 1) * 512]
                self.evac("act" if g % 2 == 0 else "dve", dst, bank[0:64, :], ["bank%d" % bi], ["wd"])
            for g in range(4):
                bi = self.nextbank()
                bank = self.banks[bi]
                for q in range(8):
                    m = g * 8 + q
                    S.op("pe", "matmul", reads=["actT4_%d" % g, "actT%d_%d" % (d, g)], writes=["bank%d" % bi], out=bank[0:64, q * 64:(q + 1) * 64],
                         lhsT=A[:, 4, m * 64:(m + 1) * 64], rhs=A[:, d, m * 64:(m + 1) * 64], start=True, stop=True)
                S.op("dve", "tensor_tensor", reads=["bank%d" % bi, "mask8_%d" % d], writes=["actT%d_%d" % (2 + d, g)],
                     out=A[0:64, 2 + d, g * 512:(g + 1) * 512].rearrange("p (a b) -> p a b", b=64), in0=bank[0:64, :].rearrange("p (a b) -> p a b", b=64),
                     in1=self.mask8[d][:].unsqueeze(1).to_broadcast([64, 8, 64]), op=ALU.mult)
        if self.hg_stop <= 4:
            return
        for d in range(2):
            S.op("pool", "memset", writes=["Sst%d" % d], ap=self.Sst[d][:], constant=0.0)
            S.op("pool", "memset", writes=["Sbf%d_0" % d], ap=self.Sbf[d][0][:], constant=0.0)
        self.rot = [0, 1, 2, 3]
        self.brr = 0
        nstep = [0, 0]
        kvbank = [None, None]
        for m in range(32):
            kvb = [self.nextbank(), self.nextbank()]
            for d in range(2):
                mt = m if d == 0 else 31 - m
                kd = self.wd[0:64, 4 * d + mt // 8, (mt % 8) * 128:(mt % 8 + 1) * 128]
                it = A[0:64, 6 + mt // 16, (mt % 16) * 128:(mt % 16 + 1) * 128]
                itk = "actT%d_%d" % (6 + mt // 16, (mt % 16) // 4)
                for h in range(2):
                    S.op("pe", "matmul", reads=["wd", itk], writes=["bank%d" % kvb[h]], out=self.banks[kvb[h]][:, d * 128:(d + 1) * 128],
                         lhsT=kd[h * 32:(h + 1) * 32, :], rhs=it[h * 32:(h + 1) * 32, :], start=True, stop=True)
            for d in range(2):
                mt = m if d == 0 else 31 - m
                grp = mt // 8
                ob_i = 4 + 2 * d + grp % 2
                ob = self.banks[ob_i]
                sl8 = mt % 8
                it = A[0:64, 6 + mt // 16, (mt % 16) * 128:(mt % 16 + 1) * 128]
                itk = "actT%d_%d" % (6 + mt // 16, (mt % 16) // 4)
                S.op("pe", "matmul", reads=[itk, "actT%d_%d" % (2 + d, mt // 8)], writes=["bank%d" % ob_i], out=ob[:, sl8 * 64:(sl8 + 1) * 64],
                     lhsT=it, rhs=A[0:64, 2 + d, mt * 64:(mt + 1) * 64], start=True, stop=False)
                order = (0, 1) if d == 0 else (1, 0)
                for oi, h in enumerate(order):
                    c = mt * 2 + h
                    n = nstep[d]
                    S.op("pe", "matmul", reads=["Sbf%d_%d" % (d, n % 3), "actT%d_%d" % (d, c // 16)], writes=["bank%d" % ob_i],
                         out=ob[:, sl8 * 64 + h * 32:sl8 * 64 + (h + 1) * 32], lhsT=self.Sbf[d][n % 3][:], rhs=A[:, d, c * 32:(c + 1) * 32],
                         start=False, stop=(oi == 1))
                    S.op("dve", "scalar_tensor_tensor", reads=["Sst%d" % d, "Ebuf%d" % d, "bank%d" % kvb[h]], writes=["Sst%d" % d], out=self.Sst[d][:],
                         in0=self.Sst[d][:], scalar=Eb[:, d, c:c + 1], in1=self.banks[kvb[h]][:, d * 128:(d + 1) * 128], op0=ALU.mult, op1=ALU.add)
                    S.op("act", "activation", reads=["Sst%d" % d], writes=["Sbf%d_%d" % (d, (n + 1) % 3)], out=self.Sbf[d][(n + 1) % 3][:],
                         in_=self.Sst[d][:], func=AF.Identity)
                    nstep[d] += 1
                done = (sl8 == 7) if d == 0 else (sl8 == 0)
                if done:
                    first = (d == 0 and grp < 2) or (d == 1 and grp >= 2)
                    dst = F0[:, grp * 512:(grp + 1) * 512]
                    if first:
                        self.evac("act", dst, ob[:, :], ["bank%d" % ob_i], ["F8_0"])
                    else:
                        S.op("dve", "tensor_tensor", reads=["bank%d" % ob_i, "F8_0"], writes=["F8_0"], out=dst, in0=ob[:, :], in1=dst, op=ALU.add)
        self.rot = list(range(8))
        if self.hg_stop <= 5:
            return
        gslot, gkey = self.load_w_cols(wsrc, [C_HGATE + j * 128])
        self.proj_fm(gslot, gkey, 0, F1, ["F8_1"], AF.Silu)
        self.gated_norm_store(F0, "F8_0", F1, "F8_1", self.pcol("hgnorm", l), dr["o_hg_s"][j], "o_hg_s%d" % j)

    def gated_norm_store(self, O, okey, G, gkey, wcol, dst_dram, dkey):
        S, A = self.S, self.actT
        for c in range(4):
            cs = slice(c * 512, (c + 1) * 512)
            S.op("act", "activation", reads=[okey], writes=["actT4_%d" % c], out=A[:, 4, cs], in_=O[:, cs], func=AF.Square)
            bi = self.nextbank()
            bank = self.banks[bi]
            S.op("pe", "matmul", reads=["onesb", "actT4_%d" % c], writes=["bank%d" % bi], out=bank[:, :], lhsT=self.onesb[:], rhs=A[:, 4, cs],
                 start=True, stop=True)
            rst = self.F8[3][:, (c % 2) * 512:(c % 2 + 1) * 512]
            rk = "F8_3"
            S.op("act", "activation", reads=["bank%d" % bi], writes=[rk], out=rst, in_=bank[:, :], func=AF.Sqrt, bias=EPS, scale=1.0 / 128)
            S.op("dve", "reciprocal", reads=[rk], writes=[rk], out=rst, in_=rst)
            S.op("dve", "scalar_tensor_tensor", reads=[okey, rk, "prm"], writes=[rk], out=rst, in0=O[:, cs], scalar=wcol, in1=rst, op0=ALU.mult, op1=ALU.mult)
            S.op("dve", "tensor_tensor", reads=[rk, gkey], writes=["actT5_%d" % c], out=A[:, 5, cs], in0=rst, in1=G[:, cs], op=ALU.mult)
        S.dma("sp", reads=self.AK(5), writes=[dkey], out=dst_dram, in_=A[:, 5, :])

    def dn_scalars(self, l):
        S, dr, hT, tk, cf = self.S, self.dr, self.hT, self.tk, self.cf
        slot, skey = self.load_w_cols(dr["w_in"][l], [C_BETA], width=16)
        bi = self.nextbank()
        bank = self.banks[bi]
        for tt in range(16):
            for kt in range(KT):
                S.op("pe", "matmul", reads=[skey] + self.hkeys(PAD + tt * 128, PAD + (tt + 1) * 128), writes=["bank%d" % bi],
                     out=bank[:, tt * 16:(tt + 1) * 16], lhsT=hT[:, kt, PAD + tt * 128:PAD + (tt + 1) * 128], rhs=slot[:, kt, 0:16],
                     start=(kt == 0), stop=(kt == KT - 1))
        b3 = bank[:, 0:256].rearrange("p (t c) -> p t c", c=16)
        allk = ["tk"]
        S.op("act", "activation", reads=["bank%d" % bi], writes=allk, out=tk["beta"][:], in_=b3[:, :, 0:8], func=AF.Sigmoid)
        S.op("dve", "tensor_scalar", reads=allk, writes=allk, out=tk["nbeta"][:], in0=tk["beta"][:], scalar1=-1.0, scalar2=None, op0=ALU.mult)
        dtb = self.pcol("dtb", l, 0, 8).unsqueeze(1).to_broadcast([128, 16, 8])
        S.op("dve", "tensor_tensor", reads=["bank%d" % bi, "prm"], writes=allk, out=tk["g"][:], in0=b3[:, :, 8:16], in1=dtb, op=ALU.add)
        S.op("act", "activation", reads=allk, writes=allk, out=tk["g"][:], in_=tk["g"][:], func=AF.Exp)
        S.op("act", "activation", reads=allk, writes=allk, out=tk["g"][:], in_=tk["g"][:], func=AF.Ln, bias=1.0, scale=1.0)
        S.op("act", "activation", reads=["prm"], writes=["nea"], out=self.nea, in_=self.pcol("alog", l, 0, 8), func=AF.Exp)
        S.op("dve", "tensor_scalar", reads=["nea"], writes=["nea"], out=self.nea, in0=self.nea, scalar1=-1.0, scalar2=None, op0=ALU.mult)
        S.op("dve", "tensor_tensor", reads=allk + ["nea"], writes=allk, out=tk["g"][:], in0=tk["g"][:],
             in1=self.nea.unsqueeze(1).to_broadcast([128, 16, 8]), op=ALU.mult)
        g2 = tk["g"][:].rearrange("p t c -> p (t c)")
        bj = self.nextbank()
        bk2 = self.banks[bj]
        S.op("pe", "matmul", reads=allk + ["cf_U2"], writes=["bank%d" % bj], out=bk2[:, 0:128], lhsT=cf["U2"][:], rhs=g2, start=True, stop=True)
        S.op("pe", "matmul", reads=allk + ["cf_L2"], writes=["bank%d" % bj], out=bk2[:, 128:256], lhsT=cf["L2"][:], rhs=g2, start=True, stop=True)
        S.op("pe", "matmul", reads=allk + ["cf_B2"], writes=["bank%d" % bj], out=bk2[:, 256:384], lhsT=cf["B2"][:], rhs=g2, start=True, stop=True)
        v3 = lambda ap: ap.rearrange("p (t c) -> p t c", c=8)
        S.op("dve", "tensor_copy", reads=["bank%d" % bj], writes=allk, out=tk["gc"][:, :, 0:4], in_=v3(bk2[:, 0:128])[:, :, 0:4])
        S.op("dve", "tensor_copy", reads=["bank%d" % bj], writes=allk, out=tk["gc"][:, :, 4:8], in_=v3(bk2[:, 128:256])[:, :, 4:8])
        S.op("dve", "tensor_copy", reads=["bank%d" % bj], writes=allk, out=tk["tot"][:], in_=v3(bk2[:, 256:384]))
        S.op("dve", "tensor_scalar", reads=allk, writes=allk, out=tk["ngc"][:], in0=tk["gc"][:], scalar1=-1.0, scalar2=None, op0=ALU.mult)
        S.op("act", "activation", reads=allk, writes=allk, out=tk["eg"][:], in_=tk["gc"][:], func=AF.Exp)
        S.op("dve", "tensor_tensor", reads=allk, writes=allk, out=tk["ekd"][:], in0=tk["tot"][:], in1=tk["gc"][:], op=ALU.subtract)
        S.op("act", "activation", reads=allk, writes=allk, out=tk["ekd"][:], in_=tk["ekd"][:], func=AF.Exp)
        t2 = tk["tot"][:].rearrange("p t c -> p (t c)")
        bq = self.nextbank()
        bk3 = self.banks[bq]
        S.op("pe", "matmul", reads=allk + ["cf_H0"], writes=["bank%d" % bq], out=bk3[:, 0:128], lhsT=cf["H0"][:], rhs=t2, start=True, stop=True)
        S.op("pe", "matmul", reads=allk + ["cf_H1"], writes=["bank%d" % bq], out=bk3[:, 128:256], lhsT=cf["H1"][:], rhs=t2, start=True, stop=True)
        S.op("act", "activation", reads=["bank%d" % bq], writes=["glb"], out=self.F8[3][:, 1024:1280], in_=bk3[:, 0:256], func=AF.Exp)

    def dn_head(self, l, j):
        S, dr, A, hT, tk, cf, mt, mtf = self.S, self.dr, self.actT, self.hT, self.tk, self.cf, self.mt, self.mtf
        F0, F1 = self.F8[0], self.F8[1]
        wsrc = dr["w_in"][l]
        slot, skey = self.load_w_cols(wsrc, [j * 128, 512 + j * 128, 1024 + j * 128])
        for xi in range(3):
            for (a, b) in pieces(SP):
                bi = self.nextbank()
                self.proj(slot, skey, xi * 128, a, b, bi)
                self.evac("act", F0[:, a:b], self.banks[bi][:, 0:b - a], ["bank%d" % bi], ["F8_0"])
            tile = xi * 4 + j
            cv = F1[:, 0:SEQ]
            S.op("dve", "tensor_scalar", reads=["F8_0", "prm"], writes=["F8_1"], out=cv, in0=F0[:, 0:SEQ], scalar1=self.pcol("dnconv", l, tile * 5),
                 scalar2=None, op0=ALU.mult)
            for k in range(1, 5):
                S.op("dve", "scalar_tensor_tensor", reads=["F8_0", "F8_1", "prm"], writes=["F8_1"], out=cv, in0=F0[:, k:k + SEQ],
                     scalar=self.pcol("dnconv", l, tile * 5 + k), in1=cv, op0=ALU.mult, op1=ALU.add)
            S.op("act", "activation", reads=["F8_1"], writes=["F8_1"], out=cv, in_=cv, func=AF.Silu)
            if xi == 2:
                S.op("dve", "tensor_copy", reads=["F8_1"], writes=self.AK(2), out=A[:, 2, :], in_=cv)
                continue
            for c in range(4):
                cs = slice(c * 512, (c + 1) * 512)
                S.op("act", "activation", reads=["F8_1"], writes=["actT7_%d" % c], out=A[:, 7, cs], in_=cv[:, cs], func=AF.Square)
                bi = self.nextbank()
                bank = self.banks[bi]
                S.op("pe", "matmul", reads=["onesb", "actT7_%d" % c], writes=["bank%d" % bi], out=bank[:, :], lhsT=self.onesb[:], rhs=A[:, 7, cs],
                     start=True, stop=True)
                rst = self.F8[3][:, (c % 2) * 512:(c % 2 + 1) * 512]
                S.op("act", "activation", reads=["bank%d" % bi], writes=["F8_3"], out=rst, in_=bank[:, :], func=AF.Sqrt, bias=EPS, scale=1.0)
                S.op("dve", "reciprocal", reads=["F8_3"], writes=["F8_3"], out=rst, in_=rst)
                S.op("dve", "scalar_tensor_tensor", reads=["F8_1", "F8_3"], writes=["actT%d_%d" % (xi, c)], out=A[:, xi, cs], in0=cv[:, cs],
                     scalar=(float(128 ** -0.5) if xi == 0 else 1.0), in1=rst, op0=ALU.mult, op1=ALU.mult)
        for (srcs, dsts) in ((1, 3), (2, 4)):
            for g in range(4):
                bi = self.nextbank()
                bank = self.banks[bi]
                for q in range(4):
                    tt = g * 4 + q
                    S.op("pe", "matmul", reads=["actT%d_%d" % (srcs, g), "identb"], writes=["bank%d" % bi], out=bank[:, q * 128:(q + 1) * 128],
                         lhsT=A[:, srcs, tt * 128:(tt + 1) * 128], rhs=self.identb[:], start=True, stop=True)
                self.evac("act" if g % 2 == 0 else "dve", A[:, dsts, g * 512:(g + 1) * 512], bank[:, :], ["bank%d" % bi], ["actT%d_%d" % (dsts, g)])
        k3 = A[:, 3, :].rearrange("p (t d) -> p t d", d=128)
        for d in range(2):
            col = d * 4 + j
            S.op("dve", "tensor_tensor", reads=self.AK(3) + ["tk"], writes=self.AK(5 + d), out=A[:, 5 + d, :].rearrange("p (t d) -> p t d", d=128),
                 in0=k3, in1=tk["ekd"][:, :, col:col + 1].to_broadcast([128, 16, 128]), op=ALU.mult)
        for d in range(2):
            col = d * 4 + j
            dg = A[:, 7, :].rearrange("p (t d) -> p t d", d=128)
            S.op("dve", "tensor_tensor", reads=["identf", "tk"], writes=self.AK(7), out=dg, in0=self.identf[:].unsqueeze(1).to_broadcast([128, 16, 128]),
                 in1=tk["eg"][:, :, col:col + 1].to_broadcast([128, 16, 128]), op=ALU.mult)
            for c in range(4):
                cs = slice(c * 512, (c + 1) * 512)
                bi = self.nextbank()
                bank = self.banks[bi]
                S.op("pe", "matmul", reads=["onesb", "actT7_%d" % c], writes=["bank%d" % bi], out=bank[:, :], lhsT=self.onesb[:], rhs=A[:, 7, cs],
                     start=True, stop=True)
                S.op("dve", "tensor_tensor", reads=["bank%d" % bi, "actT0_%d" % c], writes=["wd"], out=self.wd[:, 2 * d + c // 2, (c % 2) * 512:(c % 2 + 1) * 512],
                     in0=bank[:, :], in1=A[:, 0, cs], op=ALU.mult)
        self.barrier()
        qkv = [self.F8[2][:, 0:1024].bitcast(BF16), self.F8[3][:, 0:1024].bitcast(BF16)]
        qkk = ["F8_2", "F8_3"]
        I_b = self.identb
        self.rot = list(range(7))
        self.brr = 0
        for tt in range(16):
            ts = slice(tt * 128, (tt + 1) * 128)
            bkk = 7
            S.op("pe", "matmul", reads=["actT1_%d" % (tt // 4)], writes=["bank%d" % bkk], out=self.banks[bkk][:, 0:128], lhsT=A[:, 1, ts], rhs=A[:, 1, ts],
                 start=True, stop=True)
            S.op("pe", "matmul", reads=["actT1_%d" % (tt // 4), "actT0_%d" % (tt // 4)], writes=["bank%d" % bkk], out=self.banks[bkk][:, 128:256],
                 lhsT=A[:, 1, ts], rhs=A[:, 0, ts], start=True, stop=True)
            KK = self.banks[bkk][:, 0:128]
            KQ = self.banks[bkk][:, 128:256]
            for d in range(2):
                col = d * 4 + j
                sc = lambda nm: tk[nm][:, tt, col:col + 1]
                dn = "fb"[d]
                S.op("dve", "tensor_scalar", reads=["identf", "tk"], writes=["mtf_dg"], out=mtf["dg"][:], in0=self.identf[:], scalar1=sc("gc"), scalar2=None, op0=ALU.mult)
                bp = self.nextbank()
                P1 = self.banks[bp][:, 0:128]
                P2 = self.banks[bp][:, 128:256]
                for (Pm, M) in ((P1, "M1" + dn), (P2, "M2" + dn)):
                    S.op("pe", "matmul", reads=["cf_ones", "mtf_dg"], writes=["bank%d" % bp], out=Pm, lhsT=cf["ones"][:], rhs=mtf["dg"][:], start=True, stop=False)
                    S.op("pe", "matmul", reads=["identf", "cf_" + M], writes=["bank%d" % bp], out=Pm, lhsT=self.identf[:], rhs=cf[M][:], start=False, stop=True)
                S.op("act", "activation", reads=["bank%d" % bp, "tk"], writes=["mtf_EA"], out=mtf["EA"][:], in_=P1, func=AF.Exp, scale=-1.0, bias=sc("gc"))
                S.op("act", "activation", reads=["bank%d" % bp, "tk"], writes=["mtf_EQ"], out=mtf["EQ"][:], in_=P2, func=AF.Exp, scale=1.0, bias=sc("ngc"))
                S.op("dve", "tensor_tensor", reads=["bank%d" % bkk, "mtf_EQ"], writes=[qkk[d]], out=qkv[d][:, ts], in0=KQ, in1=mtf["EQ"][:], op=ALU.mult)
                Bd, Bo = mt["B"][:, 0:128], mt["B"][:, 128:256]
                S.op("dve", "scalar_tensor_tensor", reads=["bank%d" % bkk, "mtf_EA", "tk"], writes=["mt_B"], out=Bo, in0=KK, scalar=sc("nbeta"), in1=mtf["EA"][:],
                     op0=ALU.mult, op1=ALU.mult)
                S.op("dve", "tensor_tensor", reads=["mt_B", "mask16"], writes=["mt_B"], out=Bd, in0=Bo, in1=self.mask16[:], op=ALU.mult)
                S.op("dve", "tensor_tensor", reads=["mt_B"], writes=["mt_B"], out=Bo, in0=Bo, in1=Bd, op=ALU.subtract)
                bt = self.nextbank()
                for h in range(2):
                    S.op("pe", "matmul", reads=["mt_B", "identb"], writes=["bank%d" % bt], out=self.banks[bt][:, h * 128:(h + 1) * 128],
                         lhsT=mt["B"][:, h * 128:(h + 1) * 128], rhs=I_b[:], start=True, stop=True)
                self.evac("act", mt["BT"][:], self.banks[bt][:, 0:256], ["bank%d" % bt], ["mt_BT"])
                BdT, BoT = mt["BT"][:, 0:128], mt["BT"][:, 128:256]
                SQ, SQT = mt["SQ"], mt["SQT"]
                S.op("dve", "tensor_tensor", reads=["mt_B", "identb"], writes=["mt_SQ"], out=SQ[:, 0:128], in0=Bd, in1=I_b[:], op=ALU.add)
                S.op("dve", "tensor_tensor", reads=["mt_BT", "identb"], writes=["mt_SQT"], out=SQT[:, 0:128], in0=BdT, in1=I_b[:], op=ALU.add)
                b0 = self.nextbank()
                S.op("pe", "matmul", reads=["mt_B", "mt_BT"], writes=["bank%d" % b0], out=self.banks[b0][:, 0:128], lhsT=BdT, rhs=Bd, start=True, stop=True)
                S.op("pe", "matmul", reads=["mt_B", "mt_BT"], writes=["bank%d" % b0], out=self.banks[b0][:, 128:256], lhsT=Bd, rhs=BdT, start=True, stop=True)
                self.evac("act", SQ[:, 128:256], self.banks[b0][:, 0:128], ["bank%d" % b0], ["mt_SQ"])
                self.evac("act", SQT[:, 128:256], self.banks[b0][:, 128:256], ["bank%d" % b0], ["mt_SQT"])
                for lvl in range(3):
                    last = (lvl == 2)
                    n = 128 if last else 256
                    b1 = self.nextbank()
                    b2 = self.nextbank()
                    S.op("pe", "matmul", reads=["mt_SQ", "mt_SQT"], writes=["bank%d" % b1], out=self.banks[b1][:, 0:n], lhsT=SQT[:, 128:256], rhs=SQ[:, 0:n], start=True, stop=True)
                    S.op("pe", "matmul", reads=["mt_SQ", "mt_SQT"], writes=["bank%d" % b2], out=self.banks[b2][:, 0:n], lhsT=SQ[:, 128:256], rhs=SQT[:, 0:n], start=True, stop=True)
                    S.op("dve", "tensor_tensor", reads=["bank%d" % b1, "mt_SQ"], writes=["mt_SQ"], out=SQ[:, 0:128], in0=self.banks[b1][:, 0:128], in1=SQ[:, 0:128], op=ALU.add)
                    S.op("dve", "tensor_tensor", reads=["bank%d" % b2, "mt_SQT"], writes=["mt_SQT"], out=SQT[:, 0:128], in0=self.banks[b2][:, 0:128], in1=SQT[:, 0:128], op=ALU.add)
                    if not last:
                        self.evac("act", SQ[:, 128:256], self.banks[b1][:, 128:256], ["bank%d" % b1], ["mt_SQ"])
                        self.evac("act", SQT[:, 128:256], self.banks[b2][:, 128:256], ["bank%d" % b2], ["mt_SQT"])
                Td, TdT = SQ[:, 0:128], SQT[:, 0:128]
                bn = self.nextbank()
                S.op("pe", "matmul", reads=["mt_BT", "mt_SQ"], writes=["bank%d" % bn], out=self.banks[bn][:, 0:128], lhsT=BoT, rhs=Td, start=True, stop=True)
                self.evac("act", mt["N"][:, 0:128], self.banks[bn][:, 0:128], ["bank%d" % bn], ["mt_N"])
                Nn = mt["N"][:, 0:128]
                Y0 = TdT
                S.op("dve", "tensor_scalar", reads=["mt_SQT", "tk"], writes=["mt_Y0b"], out=mt["Y0b"][:, 0:128], in0=Y0, scalar1=sc("beta"), scalar2=None, op0=ALU.mult)
                Z = mt["Z"]
                prev, prevk = Y0, "mt_SQT"
                for it in range(3):
                    bz = self.nextbank()
                    S.op("pe", "matmul", reads=["mt_N", prevk], writes=["bank%d" % bz], out=self.banks[bz][:, 0:128], lhsT=Nn, rhs=prev, start=True, stop=True)
                    if it < 2:
                        zo = Z[:, (it % 2) * 128:(it % 2 + 1) * 128]
                        S.op("dve", "tensor_tensor", reads=["bank%d" % bz, "mt_SQT"], writes=["mt_Z"], out=zo, in0=self.banks[bz][:, 0:128], in1=Y0, op=ALU.add)
                        prev, prevk = zo, "mt_Z"
                    else:
                        TTb = self.wd[:, 4 + 2 * d + tt // 8, (tt % 8) * 128:(tt % 8 + 1) * 128]
                        S.op("dve", "scalar_tensor_tensor", reads=["bank%d" % bz, "mt_Y0b", "tk"], writes=["wd"], out=TTb, in0=self.banks[bz][:, 0:128], scalar=sc("beta"),
                             in1=mt["Y0b"][:, 0:128], op0=ALU.mult, op1=ALU.add)
                TTbg = mt["Z"][:, 0:128]
                S.op("dve", "tensor_scalar", reads=["wd", "tk"], writes=["mt_Z"], out=TTbg, in0=TTb, scalar1=sc("eg"), scalar2=None, op0=ALU.mult)
                bw = self.nextbank()
                S.op("pe", "matmul", reads=["actT3_%d" % (tt // 4), "mt_Z"], writes=["bank%d" % bw], out=self.banks[bw][:, 0:128], lhsT=A[:, 3, ts], rhs=TTbg, start=True, stop=True)
                S.op("act", "activation", reads=["bank%d" % bw], writes=["actT%d_%d" % ((2, 7)[d], tt // 4)], out=A[:, (2, 7)[d], ts], in_=self.banks[bw][:, 0:128], func=AF.Identity, scale=-1.0)
        for d in range(2):
            S.op("pool", "memset", writes=["Sst%d" % d], ap=self.Sst[d][:], constant=0.0)
            S.op("pool", "memset", writes=["Sbf%d_0" % d], ap=self.Sbf[d][0][:], constant=0.0)
        self.rot = [0, 1, 2, 3]
        self.brr = 0
        oT = self.F8[0][:, 0:SEQ]
        vnb = [self.mt["Y0"], self.mt["Y0b"]]
        for h in range(2):
            S.op("pool", "memset", writes=["vn0", "vn1", "mt_Y0b", "mt_Y0"], ap=vnb[h][:, :], constant=0.0)
        for n in range(32):
            for d in range(2):
                c = n if d == 0 else 31 - n
                tt, h = c // 2, c % 2
                ps_ = slice(h * 64, (h + 1) * 64)
                col = d * 4 + j
                grp = c // 8
                ob_i = 4 + 2 * d + grp % 2
                ob = self.banks[ob_i]
                sl8 = c % 8
                Sb = self.Sbf[d][n % 3]
                Sbk = "Sbf%d_%d" % (d, n % 3)
                cs = slice(c * 64, (c + 1) * 64)
                TTb = self.wd[:, 4 + 2 * d + tt // 8, (tt % 8) * 128:(tt % 8 + 1) * 128]
                nw = (2, 7)[d]
                bv = self.rot[h * 2 + (n % 2)]
                pv = self.banks[bv][ps_, d * 128:(d + 1) * 128]
                S.op("pe", "matmul", reads=["wd", "actT4_%d" % (tt // 4)], writes=["bank%d" % bv], out=pv, lhsT=TTb[:, h * 64:(h + 1) * 64], rhs=A[:, 4, tt * 128:(tt + 1) * 128],
                     start=True, stop=False)
                S.op("pe", "matmul", reads=["actT%d_%d" % (nw, c // 8), Sbk], writes=["bank%d" % bv], out=pv, lhsT=A[:, nw, cs], rhs=Sb[:], start=False, stop=True)
                vfull = vnb[h][:, d * 128:(d + 1) * 128]
                S.op("act", "activation", reads=["bank%d" % bv], writes=["vn%d" % d], out=vnb[h][ps_, d * 128:(d + 1) * 128], in_=pv, func=AF.Identity)
                oslot = ob[:, sl8 * 64:(sl8 + 1) * 64]
                S.op("pe", "matmul", reads=[Sbk, "wd"], writes=["bank%d" % ob_i], out=oslot, lhsT=Sb[:], rhs=self.wd[:, 2 * d + c // 16, (c % 16) * 64:(c % 16 + 1) * 64],
                     start=True, stop=False)
                S.op("pe", "matmul", reads=["vn%d" % d, qkk[d]], writes=["bank%d" % ob_i], out=oslot, lhsT=vfull, rhs=qkv[d][:, tt * 128 + h * 64:tt * 128 + (h + 1) * 64],
                     start=False, stop=True)
                psS = self.banks[bv][:, 256 + d * 128:256 + (d + 1) * 128]
                S.op("pe", "matmul", reads=["actT%d_%d" % (5 + d, tt // 4), "vn%d" % d], writes=["bank%d" % bv], out=psS, lhsT=A[:, 5 + d, tt * 128:(tt + 1) * 128], rhs=vfull,
                     start=True, stop=True)
                S.op("dve", "scalar_tensor_tensor", reads=["Sst%d" % d, "glb", "bank%d" % bv], writes=["Sst%d" % d], out=self.Sst[d][:], in0=self.Sst[d][:],
                     scalar=self.glb[:, h, tt, col:col + 1], in1=psS, op0=ALU.mult, op1=ALU.add)
                S.op("act", "activation", reads=["Sst%d" % d], writes=["Sbf%d_%d" % (d, (n + 1) % 3)], out=self.Sbf[d][(n + 1) % 3][:], in_=self.Sst[d][:], func=AF.Identity)
                done = (sl8 == 7) if d == 0 else (sl8 == 0)
                if done:
                    first = (d == 0 and grp < 2) or (d == 1 and grp >= 2)
                    dst = oT[:, grp * 512:(grp + 1) * 512]
                    if first:
                        self.evac("act", dst, ob[:, :], ["bank%d" % ob_i], ["F8_0"])
                    else:
                        S.op("dve", "tensor_tensor", reads=["bank%d" % ob_i, "F8_0"], writes=["F8_0"], out=dst, in0=ob[:, :], in1=dst, op=ALU.add)
        self.rot = list(range(8))
        if self.dbg_head is not None:
            raise StopIteration
        self.barrier()
        gslot, gkey = self.load_w_cols(wsrc, [C_Z + j * 128])
        self.proj_fm(gslot, gkey, 0, F1[:, 0:SEQ], ["F8_1"], AF.Silu)
        self.gated_norm_store(oT, "F8_0", F1[:, 0:SEQ], "F8_1", self.pcol("dnnorm", l), dr["o_dn_s"][j], "o_dn_s%d" % j)

    def merge(self, l):
        S, dr, A, xT = self.S, self.dr, self.actT, self.xT
        F0, F1 = self.F8[0][:, 0:SEQ], self.F8[1][:, 0:SEQ]
        for hd in range(4):
            if self.do_dn:
                S.dma("sp", reads=["o_dn_s%d" % hd], writes=self.AK(hd), out=A[:, hd, :], in_=dr["o_dn_s"][hd])
            else:
                S.op("pool", "memset", writes=self.AK(hd), ap=A[:, hd, :], constant=0.0)
            if self.do_hg:
                S.dma("sp", reads=["o_hg_s%d" % hd], writes=self.AK(4 + hd), out=A[:, 4 + hd, :], in_=dr["o_hg_s"][hd])
            else:
                S.op("pool", "memset", writes=self.AK(4 + hd), ap=A[:, 4 + hd, :], constant=0.0)
        mg = self.wd
        for grp in range(2):
            for q in range(4):
                ct = grp * 4 + q
                mdst = self.wd[:, 2 * q:2 * q + 2, :]
                for bidx, (gcol, wname, s0) in enumerate(((C_GDN, "w_bdn", 0), (C_GHG, "w_bhg", 4))):
                    gs, gk = self.load_w_cols(dr["w_in"][l], [gcol + ct * 128])
                    self.proj_fm(gs, gk, 0, F0, ["F8_0"], AF.Sigmoid)
                    i = self.wrr
                    self.wrr = (self.wrr + 1) % len(self.wslot)
                    bs = self.wslot[i]
                    S.dma("pool", writes=[self.WK(i)], out=bs[:, 0:4, 0:128],
                          in_=dr[wname][l, :, ct * 128:(ct + 1) * 128].rearrange("(kt p) n -> p kt n", p=128))
                    for c in range(4):
                        bi = self.nextbank()
                        bank = self.banks[bi]
                        for hd in range(4):
                            S.op("pe", "matmul", reads=[self.WK(i), "actT%d_%d" % (s0 + hd, c)], writes=["bank%d" % bi], out=bank[:, :],
                                 lhsT=bs[:, hd, 0:128], rhs=A[:, s0 + hd, c * 512:(c + 1) * 512], start=(hd == 0), stop=(hd == 3))
                        cs = slice(c * 512, (c + 1) * 512)
                        if bidx == 0:
                            S.op("dve", "tensor_tensor", reads=["bank%d" % bi, "F8_0"], writes=["F8_1"], out=F1[:, cs], in0=bank[:, :], in1=F0[:, cs], op=ALU.mult)
                        else:
                            S.op("dve", "tensor_tensor", reads=["bank%d" % bi, "F8_0"], writes=["F8_0"], out=F0[:, cs], in0=bank[:, :], in1=F0[:, cs], op=ALU.mult)
                            S.op("pool", "tensor_tensor", reads=["F8_0", "F8_1"], writes=["wd"], out=mdst[:, c // 2, (c % 2) * 512:(c % 2 + 1) * 512],
                                 in0=F0[:, cs], in1=F1[:, cs], op=ALU.add)
            i = self.wrr
            self.wrr = (self.wrr + 1) % len(self.wslot)
            ws = self.wslot[i]
            wv = ws[:, :, :].rearrange("p a b -> p (a b)")
            S.dma("pool", writes=[self.WK(i)], out=ws[:, :, :].rearrange("p (k two) n -> p k (two n)", two=2),
                  in_=dr["w_out"][l, grp * 512:(grp + 1) * 512, :].rearrange("(kt p) n -> p kt n", p=128))
            for dm in range(KT):
                for c in range(4):
                    bi = self.nextbank()
                    bank = self.banks[bi]
                    for q in range(4):
                        S.op("pe", "matmul", reads=[self.WK(i), "wd"], writes=["bank%d" % bi], out=bank[:, :],
                             lhsT=wv[:, q * 1024 + dm * 128:q * 1024 + (dm + 1) * 128],
                             rhs=self.wd[:, 2 * q + c // 2, (c % 2) * 512:(c % 2 + 1) * 512], start=(q == 0), stop=(q == 3))
                    S.op("dve", "tensor_tensor", reads=["bank%d" % bi, "xT%d_%d" % (dm, c)], writes=["xT%d_%d" % (dm, c)],
                         out=xT[:, dm, c * 512:(c + 1) * 512], in0=bank[:, :], in1=xT[:, dm, c * 512:(c + 1) * 512], op=ALU.add)

    def final_store(self, s):
        S, dr, xT, identf = self.S, self.dr, self.xT, self.identf
        for c in range(4):
            rst = self.F8[3][:, (c % 2) * 512:(c % 2 + 1) * 512]
            sqv = self.actT[:, 0:8, c * 512:(c + 1) * 512]
            self.rstd_chunk(c, rst, "F8_3", sqv, "sq%d_" % c)
            for kt in range(KT):
                ybuf = self.F8[kt // 4]
                S.op("dve", "scalar_tensor_tensor", reads=["xT%d_%d" % (kt, c), "F8_3", "prm"], writes=["F8_%d" % (kt // 4)],
                     out=ybuf[:, (kt % 4) * 512:(kt % 4 + 1) * 512], in0=xT[:, kt, c * 512:(c + 1) * 512],
                     scalar=self.pcol("finaln", None, kt), in1=rst, op0=ALU.mult, op1=ALU.mult)
            for t4 in range(4):
                tt = c * 4 + t4
                otile = self.F8[2]
                okey = "F8_2"
                for h2 in range(2):
                    bi = self.nextbank()
                    bank = self.banks[bi]
                    for j in range(4):
                        kt = h2 * 4 + j
                        ybuf = self.F8[kt // 4]
                        S.op("pe", "transpose", reads=["F8_%d" % (kt // 4), "identf"], writes=["bank%d" % bi],
                             out=bank[:, j * 128:(j + 1) * 128],
                             in_=ybuf[:, (kt % 4) * 512 + t4 * 128:(kt % 4) * 512 + (t4 + 1) * 128], identity=identf[:])
                    self.evac("act", otile[:, h2 * 512:(h2 + 1) * 512], bank[:, :], ["bank%d" % bi], [okey])
                S.dma("sp", reads=[okey], writes=["out"], out=dr["out"][s, tt * 128:(tt + 1) * 128, :], in_=otile[:, 0:D])


_CACHE = {}


def get_nc(nseq, **kw):
    key = (nseq, tuple(sorted(kw.items())))
    if key not in _CACHE:
        _CACHE[key] = Builder(nseq, **kw).build()
    return _CACHE[key]


def kernel(x, mix_norm, w_in, dn_conv, dn_a_log, dn_dt_bias, dn_norm, hg_lb_logits, hg_norm, w_branch_dn, w_branch_hg,
           w_out, ffn_norm, w_up, ffn_conv, ffn_conv_bias, w_down, final_norm):
    f = lambda a: np.ascontiguousarray(np.asarray(a, dtype=np.float32))
    x = f(x)
    P = pack_params(f(mix_norm), f(dn_conv), f(dn_a_log), f(dn_dt_bias), f(dn_norm), f(hg_lb_logits), f(hg_norm),
                    f(ffn_norm), f(ffn_conv), f(ffn_conv_bias), f(final_norm))
    nseq = x.shape[0] // NCORES
    nc = get_nc(nseq)
    shared = {"params": P, "w_in": f(w_in), "w_branch_dn": f(w_branch_dn), "w_branch_hg": f(w_branch_hg),
              "w_out": f(w_out), "w_up": f(w_up), "w_down": f(w_down)}
    in_maps = []
    for c in range(NCORES):
        m = dict(shared)
        m["x"] = x[c * nseq:(c + 1) * nseq]
        in_maps.append(m)
    res = run_bass_kernel_spmd(nc, in_maps, core_ids=list(range(NCORES)))
    return np.concatenate([r["out"] for r in res.results], axis=0)
```

```python
import numpy as np
import concourse.bass as bass
import concourse.mybir as mybir
from concourse.bass_utils import run_bass_kernel_spmd
from contextlib import ExitStack

F32 = mybir.dt.float32
BF16 = mybir.dt.bfloat16
AF = mybir.ActivationFunctionType
ALU = mybir.AluOpType

D = 1024
KT = 8
SEQ = 2048
PAD = 2
SP = SEQ + 2 * PAD
DEPTH = 2
DN_W = 512
INC = 6672
DFF = 2816
FT = 22
EPS = 1e-6
NCORES = 8
C_QKV = 0
C_Z = 1536
C_BETA = 2048
C_A = 2056
C_HQ = 2064
C_HF = (2576, 3088)
C_HI = 3600
C_HGATE = 4112
C_GDN = 4624
C_GHG = 5648

ENGS = ("pe", "act", "dve", "pool", "sp")
NDMA_SEM = 24
EPOCH = 4000


class Sched:
    def __init__(self, nc, stack, same_engine_sync=True):
        self.nc = nc
        self.stack = stack
        self.sem = {e: [] for e in ENGS}
        self.cnt = {e: 0 for e in ENGS}
        self.seen = {e: {} for e in ENGS}
        self.ops = {e: [] for e in ENGS}
        self.lastw = {}
        self.readers = {}
        self.dma_sems = [stack.enter_context(nc.semaphore("s_dma%d" % i)) for i in range(NDMA_SEM)]
        self.dma_uses = [0] * NDMA_SEM
        self.dma_rr = 0
        self.dma_rr_sw = 0
        self.same = same_engine_sync
        self.semobj = {}
        for i in range(NDMA_SEM):
            self.semobj[("d", i)] = self.dma_sems[i]

    def _deps(self, eng, reads, writes):
        need = {}

        def add(tok):
            if tok is None:
                return
            k, v = tok
            if k == ("e", eng) and (not self.same or eng == "pe"):
                return
            if need.get(k, 0) < v:
                need[k] = v
        for k in reads:
            add(self.lastw.get(k))
        for k in writes:
            add(self.lastw.get(k))
            for t in self.readers.get(k, {}).items():
                add(t)
        out = []
        for k, v in need.items():
            if self.seen[eng].get(k, 0) >= v:
                continue
            self.seen[eng][k] = v
            out.append((k, v))
        return out

    def _record(self, tok, reads, writes):
        for k in reads:
            d = self.readers.setdefault(k, {})
            if d.get(tok[0], 0) < tok[1]:
                d[tok[0]] = tok[1]
        for k in writes:
            self.lastw[k] = tok
            self.readers[k] = {}

    @staticmethod
    def _flat(keys):
        out = []
        for k in keys:
            if isinstance(k, (list, tuple)):
                out.extend(Sched._flat(k))
            else:
                out.append(k)
        return out

    def op(self, eng, name, reads=(), writes=(), **kw):
        reads, writes = self._flat(reads), self._flat(writes)
        fn = (name, kw)
        waits = self._deps(eng, reads, writes)
        self.cnt[eng] += 1
        tok = (("e", eng), self.cnt[eng])
        ep = (self.cnt[eng] - 1) // EPOCH
        while len(self.sem[eng]) <= ep:
            self.sem[eng].append(self.stack.enter_context(self.nc.semaphore("s_%s_%d" % (eng, len(self.sem[eng])))))
        self.ops[eng].append((waits, fn, (self.sem[eng][ep], 1)))
        self._record(tok, reads, writes)
        return tok

    def dma(self, eng, reads=(), writes=(), **kw):
        reads, writes = self._flat(reads), self._flat(writes)
        fn = ("dma_start", kw)
        half = NDMA_SEM // 2
        if eng == "pool":
            i = half + self.dma_rr_sw
            self.dma_rr_sw = (self.dma_rr_sw + 1) % half
        else:
            i = self.dma_rr
            self.dma_rr = (self.dma_rr + 1) % half
        waits = self._deps(eng, reads, writes)
        prev = self.dma_uses[i]
        k = ("d", i)
        if prev > 0 and self.seen[eng].get(k, 0) < 16 * prev:
            self.seen[eng][k] = 16 * prev
            waits.append((k, 16 * prev))
        self.dma_uses[i] += 1
        tok = (k, 16 * self.dma_uses[i])
        self.ops[eng].append((waits, fn, (self.dma_sems[i], 16)))
        self._record(tok, reads, writes)
        return tok

    def emit(self):
        nc = self.nc
        toks = []
        for e in ENGS:
            if self.cnt[e]:
                toks.append((("e", e), self.cnt[e]))
        for i in range(NDMA_SEM):
            if self.dma_uses[i]:
                toks.append((("d", i), 16 * self.dma_uses[i]))
        self.ops["sp"].append((toks, None, None))
        with nc.Block() as block:
            def replay(name):
                def body(eng):
                    for waits, fn, inc in self.ops[name]:
                        for k, v in waits:
                            if k[0] == "e":
                                ep = (v - 1) // EPOCH
                                eng.wait_ge(self.sem[k[1]][ep], v - ep * EPOCH)
                            else:
                                eng.wait_ge(self.semobj[k], v)
                        if fn is not None:
                            getattr(eng, fn[0])(**fn[1]).then_inc(inc[0], inc[1])
                return body
            block.tensor(replay("pe"))
            block.scalar(replay("act"))
            block.vector(replay("dve"))
            block.gpsimd(replay("pool"))
            block.sync(replay("sp"))


PC = {}
_off = 0
for l in range(DEPTH):
    for nm, n in (("mixn", 8), ("ffnn", 8), ("dnconv", 60), ("ffnconv", 66), ("ffnbias", 22),
                  ("dnnorm", 1), ("hgnorm", 1), ("alog", 8), ("dtb", 8)):
        PC[(nm, l)] = _off
        _off += n
PC["finaln"] = _off
_off += 8
PC["lblog"] = _off
_off += 16
NPC = _off


def pack_params(mix_norm, dn_conv, dn_a_log, dn_dt_bias, dn_norm, hg_lb_logits, hg_norm, ffn_norm, ffn_conv,
                ffn_conv_bias, final_norm):
    P = np.zeros((128, NPC), np.float32)
    for l in range(DEPTH):
        P[:, PC[("mixn", l)]:PC[("mixn", l)] + 8] = mix_norm[l].reshape(8, 128).T
        P[:, PC[("ffnn", l)]:PC[("ffnn", l)] + 8] = ffn_norm[l].reshape(8, 128).T
        P[:, PC[("dnconv", l)]:PC[("dnconv", l)] + 60] = dn_conv[l].reshape(5, 12, 128).transpose(2, 1, 0).reshape(128, 60)
        P[:, PC[("ffnconv", l)]:PC[("ffnconv", l)] + 66] = ffn_conv[l].reshape(3, 22, 128).transpose(2, 1, 0).reshape(128, 66)
        P[:, PC[("ffnbias", l)]:PC[("ffnbias", l)] + 22] = ffn_conv_bias[l].reshape(22, 128).T
        P[:, PC[("dnnorm", l)]] = dn_norm[l]
        P[:, PC[("hgnorm", l)]] = hg_norm[l]
        P[:, PC[("alog", l)]:PC[("alog", l)] + 8] = np.broadcast_to(dn_a_log[l].reshape(1, 8), (128, 8))
        P[:, PC[("dtb", l)]:PC[("dtb", l)] + 8] = np.broadcast_to(dn_dt_bias[l].reshape(1, 8), (128, 8))
    P[:, PC["finaln"]:PC["finaln"] + 8] = final_norm.reshape(8, 128).T
    P[:, PC["lblog"]:PC["lblog"] + 16] = hg_lb_logits.reshape(2, 2, 4, 128).transpose(3, 0, 1, 2).reshape(128, 16)
    return P


def pieces(total, maxn=512):
    n = (total + maxn - 1) // maxn
    base = total // n
    rem = total - base * n
    out = []
    a = 0
    for i in range(n):
        b = a + base + (1 if i < rem else 0)
        out.append((a, b))
        a = b
    return out


class Builder:
    def __init__(self, nseq, depth=DEPTH, do_mixer=True, do_ffn=True, do_dn=True, do_hg=True, debug=None, hg_stop=99, dbg_head=None):
        self.nseq = nseq
        self.depth = depth
        self.do_mixer = do_mixer
        self.do_ffn = do_ffn
        self.do_dn = do_dn
        self.do_hg = do_hg
        self.hg_stop = hg_stop
        self.dbg_head = dbg_head
        self.debug = debug or {}
        self.nc = bass.Bass("TRN2", target_bir_lowering=False)

    def sb(self, name, shape, dt):
        return self.st.enter_context(self.nc.sbuf_tensor(name, shape, dt))

    def hkeys(self, a, b):
        lo = max(a - PAD, 0)
        hi = min(b - PAD, SEQ)
        if hi <= lo:
            return []
        return ["hT%d" % c for c in range(lo // 512, (hi - 1) // 512 + 1)]

    def build(self):
        nc = self.nc
        nseq = self.nseq
        with ExitStack() as st:
            self.st = st
            S = self.S = Sched(nc, st)
            dr = {}
            dr["x"] = nc.dram_tensor("x", [nseq, SEQ, D], F32, kind="ExternalInput").ap()
            dr["params"] = nc.dram_tensor("params", [128, NPC], F32, kind="ExternalInput").ap()
            dr["w_in"] = nc.dram_tensor("w_in", [DEPTH, D, INC], F32, kind="ExternalInput").ap()
            dr["w_bdn"] = nc.dram_tensor("w_branch_dn", [DEPTH, DN_W, D], F32, kind="ExternalInput").ap()
            dr["w_bhg"] = nc.dram_tensor("w_branch_hg", [DEPTH, DN_W, D], F32, kind="ExternalInput").ap()
            dr["w_out"] = nc.dram_tensor("w_out", [DEPTH, D, D], F32, kind="ExternalInput").ap()
            dr["w_up"] = nc.dram_tensor("w_up", [DEPTH, D, 2 * DFF], F32, kind="ExternalInput").ap()
            dr["w_down"] = nc.dram_tensor("w_down", [DEPTH, DFF, D], F32, kind="ExternalInput").ap()
            dr["out"] = nc.dram_tensor("out", [nseq, SEQ, D], F32, kind="ExternalOutput").ap()
            for k, shp in self.debug.items():
                dr[k] = nc.dram_tensor(k, list(shp), F32, kind="ExternalOutput").ap()
            self.dr = dr

            self.xT = self.sb("xT", [128, KT, SEQ], F32)
            self.hT = self.sb("hT", [128, KT, SP], BF16)
            self.prm = self.sb("prm", [128, NPC], F32)
            self.identf = self.sb("identf", [128, 128], F32)
            self.identb = self.sb("identb", [128, 128], BF16)
            self.onesb = self.sb("onesb", [128, 128], BF16)
            self.wslot = [self.sb("wslot%d" % i, [128, KT, 512], BF16) for i in range(2)]
            self.wrr = 0
            self.F8 = [self.sb("F8_%d" % i, [128, SP], F32) for i in range(4)]
            self.actT = self.sb("actT", [128, 8, SEQ], BF16)
            self.wd = self.sb("wd", [128, 8, D], BF16)
            self.rmask32 = self.sb("rmask32", [128, SEQ], mybir.dt.float8e4)
            self.mask8 = [self.sb("mask8_%d" % d, [64, 64], BF16) for d in range(2)]
            self.lbT = self.sb("lbT", [128, 2, 16], F32)
            self.Ebuf = self.sb("Ebuf", [128, 2, 64], F32)
            self.totb = self.sb("totb", [128, 64], F32)
            self.Sst = [self.sb("Sst%d" % d, [128, 128], F32) for d in range(2)]
            self.Sbf = [[self.sb("Sbf%d_%d" % (d, r), [128, 128], BF16) for r in range(3)] for d in range(2)]
            self.cf = {nm: self.sb("cf_" + nm, [128, 128], F32) for nm in ("U2", "L2", "B2", "H0", "H1", "ones", "M1f", "M1b", "M2f", "M2b")}
            self.mask16 = self.sb("mask16", [128, 128], BF16)
            self.dummy = self.sb("dummyb", [128, 2], F32)
            tkn = ("beta", "nbeta", "g", "gc", "ngc", "tot", "eg", "ekd")
            self.tk = {nm: self.F8[2][:, 1024 + i * 128:1024 + (i + 1) * 128].rearrange("p (t c) -> p t c", c=8) for i, nm in enumerate(tkn)}
            self.glb = self.F8[3][:, 1024:1280].rearrange("p (h t c) -> p h t c", h=2, t=16)
            self.mtf = {nm: self.F8[3][:, 1280 + i * 128:1280 + (i + 1) * 128] for i, nm in enumerate(("dg", "EA", "EQ"))}
            self.nea = self.F8[3][:, 1664:1672]
            mtv = self.F8[1][:, 0:1024].bitcast(BF16)
            self.mt = {nm: mtv[:, i * 256:(i + 1) * 256] for i, nm in enumerate(("B", "BT", "SQ", "SQT", "N", "Z", "Y0", "Y0b"))}
            self.dn_small_keys = ["tk", "glb", "nea", "mtf_dg", "mtf_EA", "mtf_EQ", "vn0", "vn1"] + ["mt_" + nm for nm in ("B", "BT", "SQ", "SQT", "N", "Z", "Y0", "Y0b")]
            dr["o_dn_s"] = nc.dram_tensor("o_dn_s", [4, 128, SEQ], BF16, kind="Internal").ap()
            dr["o_hg_s"] = nc.dram_tensor("o_hg_s", [4, 128, SEQ], BF16, kind="Internal").ap()
            self.banks = [st.enter_context(nc.psum_tensor("bank%d" % i, [128, 512], F32)) for i in range(8)]
            self.brr = 0
            self.rot = list(range(8))

            self.setup_consts()
            try:
                self.body()
            except StopIteration:
                pass
            S.emit()
        return nc

    def body(self):
        if True:
            nseq = self.nseq
            for s in range(nseq):
                self.load_x(s)
                for l in range(self.depth):
                    if self.do_mixer:
                        self.mixer(l)
                    if self.do_ffn:
                        self.ffn(l)
                self.final_store(s)

    def nextbank(self):
        self.brr = (self.brr + 1) % len(self.rot)
        return self.rot[self.brr]

    def setup_consts(self):
        S = self.S
        prm, dr = self.prm, self.dr
        S.dma("sp", writes=["prm"], out=prm[:], in_=dr["params"][:, :])
        identf, identb, onesb, hT = self.identf, self.identb, self.onesb, self.hT
        S.op("pool", "memset", writes=["identf"], ap=identf[:], constant=1.0)
        S.op("pool", "affine_select", reads=["identf"], writes=["identf"], out=identf[:], in_=identf[:], pattern=[[-1, 128]],
             compare_op=ALU.is_equal, fill=0.0, base=0, channel_multiplier=1)
        S.op("dve", "tensor_copy", reads=["identf"], writes=["identb"], out=identb[:], in_=identf[:])
        S.op("dve", "memset", writes=["onesb"], ap=onesb[:], constant=1.0)
        S.op("dve", "memset", writes=["hTpad"], ap=hT[:, :, 0:PAD], constant=0.0)
        S.op("dve", "memset", writes=["hTpad"], ap=hT[:, :, PAD + SEQ:SP], constant=0.0)
        rm = self.rmask32
        S.op("pool", "memset", writes=["rmask32"], ap=rm[:], constant=1.0)
        S.op("pool", "memset", reads=["rmask32"], writes=["rmask32"], ap=rm[:, 0:SEQ:32], constant=0.0)
        for d in range(2):
            mk = self.mask8[d]
            key = "mask8_%d" % d
            S.op("pool", "memset", writes=[key], ap=mk[:], constant=1.0)
            sg = 1 if d == 0 else -1
            S.op("pool", "affine_select", reads=[key], writes=[key], out=mk[:, :], in_=mk[:, :],
                 pattern=[[sg, 64]], compare_op=ALU.is_ge, fill=0.0, base=0, channel_multiplier=-sg)
            if d == 0:
                S.op("pool", "memset", reads=[key], writes=[key], ap=mk[0:32, 32:64], constant=0.0)
            else:
                S.op("pool", "memset", reads=[key], writes=[key], ap=mk[32:64, 0:32], constant=0.0)
        cf = self.cf
        BIG = 30000.0
        def tri(nm, sg, val_in, val_out, extra_zero=None, incl=True):
            t = cf[nm]
            S.op("pool", "memset", writes=["cf_" + nm], ap=t[:], constant=val_in)
            S.op("pool", "affine_select", reads=["cf_" + nm], writes=["cf_" + nm], out=t[:, :], in_=t[:, :], pattern=[[sg, 128]],
                 compare_op=ALU.is_ge, fill=val_out, base=(0 if incl else -1), channel_multiplier=-sg)
            S.op("pool", "memset", reads=["cf_" + nm], writes=["cf_" + nm], ap=t[0:64, 64:128], constant=val_out)
            S.op("pool", "memset", reads=["cf_" + nm], writes=["cf_" + nm], ap=t[64:128, 0:64], constant=val_out)
        tri("U2", 1, 1.0, 0.0)
        tri("L2", -1, 1.0, 0.0)
        tri("M1f", -1, 0.0, BIG, incl=False)
        tri("M1b", 1, 0.0, BIG, incl=False)
        tri("M2f", 1, 0.0, -BIG)
        tri("M2b", -1, 0.0, -BIG)
        S.op("pool", "memset", writes=["cf_B2"], ap=cf["B2"][:], constant=1.0)
        S.op("pool", "memset", reads=["cf_B2"], writes=["cf_B2"], ap=cf["B2"][0:64, 64:128], constant=0.0)
        S.op("pool", "memset", reads=["cf_B2"], writes=["cf_B2"], ap=cf["B2"][64:128, 0:64], constant=0.0)
        S.op("pool", "memset", writes=["cf_ones"], ap=cf["ones"][:], constant=1.0)
        for nm, r0 in (("H0", 0), ("H1", 64)):
            S.op("pool", "memset", writes=["cf_" + nm], ap=cf[nm][:], constant=0.0)
            S.op("pool", "memset", reads=["cf_" + nm], writes=["cf_" + nm], ap=cf[nm][r0:r0 + 1, :], constant=1.0)
        m16 = self.mask16
        S.op("pool", "memset", writes=["mask16"], ap=m16[:], constant=0.0)
        for b in range(8):
            p0 = (b * 16) // 32 * 32
            pass
        S.op("pool", "memset", reads=["mask16"], writes=["mask16"], ap=m16[:], constant=1.0)
        for b in range(8):
            S.op("pool", "affine_select", reads=["mask16"], writes=["mask16"], out=m16[:, b * 16:(b + 1) * 16], in_=m16[:, b * 16:(b + 1) * 16],
                 pattern=[[0, 16]], compare_op=ALU.is_ge, fill=0.0, base=-16 * b, channel_multiplier=1)
            S.op("pool", "affine_select", reads=["mask16"], writes=["mask16"], out=m16[:, b * 16:(b + 1) * 16], in_=m16[:, b * 16:(b + 1) * 16],
                 pattern=[[0, 16]], compare_op=ALU.is_ge, fill=0.0, base=16 * b + 15, channel_multiplier=-1)
        lbT = self.lbT
        o = PC["lblog"]
        S.op("dve", "memset", writes=["lbT"], ap=lbT[:, 0, 0:8], constant=0.0)
        S.op("dve", "tensor_tensor", reads=["prm", "lbT"], writes=["lbT"], out=lbT[:, 0, 8:16], in0=prm[:, o + 8:o + 16], in1=prm[:, o:o + 8], op=ALU.subtract)
        S.op("act", "activation", reads=["lbT"], writes=["lbT"], out=lbT[:, 0, 8:16], in_=lbT[:, 0, 8:16], func=AF.Sigmoid)
        S.op("dve", "tensor_scalar", reads=["lbT"], writes=["lbT"], out=lbT[:, 1, :], in0=lbT[:, 0, :], scalar1=-1.0, scalar2=1.0, op0=ALU.mult, op1=ALU.add)

    def pcol(self, name, l=None, j=0, n=1):
        o = PC[(name, l)] if l is not None else PC[name]
        return self.prm[:, o + j:o + j + n]

    def evac(self, eng, out, in_, reads, writes):
        if eng == "act":
            self.S.op("act", "activation", reads=reads, writes=writes, out=out, in_=in_, func=AF.Identity)
        else:
            self.S.op(eng, "tensor_copy", reads=reads, writes=writes, out=out, in_=in_)

    def load_x(self, s):
        S, dr, xT, identf = self.S, self.dr, self.xT, self.identf
        for g in range(4):
            for i in range(4):
                tt = g * 4 + i
                S.dma("sp", writes=["F8_%d" % (i // 2)], out=self.F8[i // 2][:, (i % 2) * D:(i % 2 + 1) * D], in_=dr["x"][s, tt * 128:(tt + 1) * 128, :])
            for kt in range(KT):
                b = self.nextbank()
                bank = self.banks[b]
                for i in range(4):
                    S.op("pe", "transpose", reads=["F8_%d" % (i // 2), "identf"], writes=["bank%d" % b],
                         out=bank[:, i * 128:(i + 1) * 128], in_=self.F8[i // 2][:, (i % 2) * D + kt * 128:(i % 2) * D + (kt + 1) * 128], identity=identf[:])
                self.evac("act" if kt % 2 == 0 else "dve", xT[:, kt, g * 512:(g + 1) * 512], bank[:, :],
                          ["bank%d" % b], ["xT%d_%d" % (kt, g)])

    def rstd_chunk(self, c, dst, dstkey, sqbuf, sqkey):
        S, xT, onesb = self.S, self.xT, self.onesb
        b = self.nextbank()
        bank = self.banks[b]
        for kt in range(KT):
            S.op("act", "activation", reads=["xT%d_%d" % (kt, c)], writes=["actT%d_%d" % (kt, c)],
                 out=sqbuf[:, kt, :], in_=xT[:, kt, c * 512:(c + 1) * 512], func=AF.Square)
            S.op("pe", "matmul", reads=["onesb", "actT%d_%d" % (kt, c)], writes=["bank%d" % b],
                 out=bank[:, :], lhsT=onesb[:], rhs=sqbuf[:, kt, :], start=(kt == 0), stop=(kt == KT - 1))
        S.op("act", "activation", reads=["bank%d" % b], writes=[dstkey], out=dst, in_=bank[:, :], func=AF.Sqrt, bias=EPS, scale=1.0 / D)
        S.op("dve", "reciprocal", reads=[dstkey], writes=[dstkey], out=dst, in_=dst)

    def rmsnorm_to_hT(self, wname, l):
        S, xT, hT = self.S, self.xT, self.hT
        for c in range(4):
            rst = self.F8[3][:, (c % 2) * 512:(c % 2 + 1) * 512]
            sqv = self.actT[:, 0:8, c * 512:(c + 1) * 512]
            self.rstd_chunk(c, rst, "F8_3", sqv, "sq%d_" % c)
            for kt in range(KT):
                S.op("dve", "scalar_tensor_tensor", reads=["xT%d_%d" % (kt, c), "F8_3", "prm"], writes=["hT%d" % c],
                     out=hT[:, kt, PAD + c * 512:PAD + (c + 1) * 512], in0=xT[:, kt, c * 512:(c + 1) * 512],
                     scalar=self.pcol(wname, l, kt), in1=rst, op0=ALU.mult, op1=ALU.mult)

    def load_w(self, src_ap, ncols, nk=KT):
        i = self.wrr
        self.wrr = (self.wrr + 1) % len(self.wslot)
        slot = self.wslot[i]
        self.S.dma("pool", writes=[self.WK(i)], out=slot[:, 0:nk, 0:ncols], in_=src_ap.rearrange("(kt p) n -> p kt n", p=128))
        return slot, self.WK(i)

    def proj(self, slot, skey, col, a, b, bank_i, nk=KT, src=None, srckeys=None, ncol=128):
        bank = self.banks[bank_i]
        src = self.hT if src is None else src
        keys = (self.hkeys(a, b) + ["hTpad"]) if srckeys is None else srckeys
        for kt in range(nk):
            self.S.op("pe", "matmul", reads=[skey] + keys, writes=["bank%d" % bank_i],
                      out=bank[0:ncol, 0:b - a], lhsT=slot[:, kt, col:col + ncol], rhs=src[:, kt, a:b],
                      start=(kt == 0), stop=(kt == nk - 1))

    def ffn(self, l):
        S, dr, xT, actT, wd = self.S, self.dr, self.xT, self.actT, self.wd
        self.rmsnorm_to_hT("ffnn", l)
        gpre, cv = self.F8[0], self.F8[1]
        sl = cv
        for (h0, h1) in [(0, 8), (8, 16), (16, 22)]:
            nh = h1 - h0
            S.dma("pool", writes=["wd"], out=wd[:, 0:nh, :],
                  in_=dr["w_down"][l, h0 * 128:h1 * 128, :].rearrange("(kt p) n -> p kt n", p=128))
            for g0 in range(h0, h1, 4):
                g1 = min(g0 + 4, h1)
                ng = g1 - g0
                gslot, gkey = self.load_w(dr["w_up"][l, :, g0 * 128:g1 * 128], ng * 128)
                uslot, ukey = self.load_w(dr["w_up"][l, :, DFF + g0 * 128:DFF + g1 * 128], ng * 128)
                for f in range(g0, g1):
                    fi = f - h0
                    col = (f - g0) * 128
                    for (a, b) in pieces(SP):
                        bi = self.nextbank()
                        self.proj(gslot, gkey, col, a, b, bi)
                        self.evac("act", gpre[:, a:b], self.banks[bi][:, 0:b - a], ["bank%d" % bi], ["F8_0"])
                    S.op("dve", "tensor_scalar", reads=["F8_0", "prm"], writes=["F8_1"], out=cv[:, 0:SEQ], in0=gpre[:, 1:1 + SEQ],
                         scalar1=self.pcol("ffnconv", l, f * 3), scalar2=self.pcol("ffnbias", l, f), op0=ALU.mult, op1=ALU.add)
                    for k in (1, 2):
                        S.op("dve", "scalar_tensor_tensor", reads=["F8_0", "F8_1", "prm"], writes=["F8_1"], out=cv[:, 0:SEQ],
                             in0=gpre[:, 1 + k:1 + k + SEQ], scalar=self.pcol("ffnconv", l, f * 3 + k), in1=cv[:, 0:SEQ],
                             op0=ALU.mult, op1=ALU.add)
                    S.op("act", "activation", reads=["F8_1"], writes=["F8_1"], out=sl[:, 0:SEQ], in_=cv[:, 0:SEQ], func=AF.Silu)
                    for c in range(4):
                        bi = self.nextbank()
                        self.proj(uslot, ukey, col, PAD + c * 512, PAD + (c + 1) * 512, bi)
                        S.op("dve", "tensor_tensor", reads=["bank%d" % bi, "F8_1"], writes=["actT%d_%d" % (fi, c)],
                             out=actT[:, fi, c * 512:(c + 1) * 512], in0=self.banks[bi][:, :], in1=sl[:, c * 512:(c + 1) * 512], op=ALU.mult)
            for dm in range(KT):
                for c in range(4):
                    bi = self.nextbank()
                    bank = self.banks[bi]
                    for kt in range(nh):
                        S.op("pe", "matmul", reads=["wd", "actT%d_%d" % (kt, c)], writes=["bank%d" % bi], out=bank[:, :],
                             lhsT=wd[:, kt, dm * 128:(dm + 1) * 128], rhs=actT[:, kt, c * 512:(c + 1) * 512], start=(kt == 0), stop=(kt == nh - 1))
                    S.op("dve", "tensor_tensor", reads=["bank%d" % bi, "xT%d_%d" % (dm, c)], writes=["xT%d_%d" % (dm, c)],
                         out=xT[:, dm, c * 512:(c + 1) * 512], in0=bank[:, :], in1=xT[:, dm, c * 512:(c + 1) * 512], op=ALU.add)

    def WK(self, i):
        return ["wslot%d_%d" % (i, q) for q in range(4)]

    def AK(self, i):
        return ["actT%d_%d" % (i, c) for c in range(4)]

    def load_w_cols(self, wsrc, cols, width=128):
        i = self.wrr
        self.wrr = (self.wrr + 1) % len(self.wslot)
        slot = self.wslot[i]
        for q, c0 in enumerate(cols):
            self.S.dma("pool", writes=([self.WK(i)[q]] if width == 128 else [self.WK(i)]), out=slot[:, :, q * width:(q + 1) * width],
                       in_=wsrc[:, c0:c0 + width].rearrange("(kt p) n -> p kt n", p=128))
        return slot, self.WK(i)

    def proj_fm(self, slot, skey, col, dst, dkeys, func, scale=1.0):
        for c in range(4):
            bi = self.nextbank()
            self.proj(slot, skey, col, PAD + c * 512, PAD + (c + 1) * 512, bi)
            self.S.op("act", "activation", reads=["bank%d" % bi], writes=dkeys, out=dst[:, c * 512:(c + 1) * 512],
                      in_=self.banks[bi][:, :], func=func, scale=scale)

    def mixer(self, l):
        self.rmsnorm_to_hT("mixn", l)
        if self.do_dn:
            self.barrier()
            self.dn_scalars(l)
            for j in (range(4) if self.dbg_head is None else (self.dbg_head,)):
                self.dn_head(l, j)
            if self.dbg_head is not None:
                raise StopIteration
            self.barrier()
        if self.do_hg:
            for j in range(4):
                self.hg_head(l, j)
        self.merge(l)

    def barrier(self):
        keys = ["F8_1", "F8_2", "F8_3"] + self.dn_small_keys
        self.S.op("dve", "memset", reads=[], writes=keys + ["dummy"], ap=self.dummy[:], constant=0.0)

    def hg_head(self, l, j):
        S, dr, A, hT = self.S, self.dr, self.actT, self.hT
        F0, F1, F2, F3 = [f[:, 0:SEQ] for f in self.F8]
        wsrc = dr["w_in"][l]
        slot, skey = self.load_w_cols(wsrc, [C_HQ + j * 128, C_HF[0] + j * 128, C_HF[1] + j * 128, C_HI + j * 128])
        self.proj_fm(slot, skey, 0, F0, ["F8_0"], AF.Silu)
        def itok(m):
            return A[0:64, 6 + m // 16, (m % 16) * 128:(m % 16 + 1) * 128], "actT%d_%d" % (6 + m // 16, (m % 16) // 4)
        for g in range(8):
            bi = self.nextbank()
            bank = self.banks[bi]
            for q in range(4):
                m = g * 4 + q
                for kt in range(KT):
                    S.op("pe", "matmul", reads=[skey] + self.hkeys(PAD + m * 64, PAD + (m + 1) * 64), writes=["bank%d" % bi],
                         out=bank[0:64, q * 128:(q + 1) * 128], lhsT=hT[:, kt, PAD + m * 64:PAD + (m + 1) * 64],
                         rhs=slot[:, kt, 3 * 128:4 * 128], start=(kt == 0), stop=(kt == KT - 1))
            m0 = g * 4
            dst = A[0:64, 6 + m0 // 16, (m0 % 16) * 128:(m0 % 16) * 128 + 512]
            self.evac("act" if g % 2 == 0 else "dve", dst, bank[0:64, :], ["bank%d" % bi], ["actT%d_%d" % (6 + m0 // 16, (m0 % 16) // 4)])
        Eb, totb = self.Ebuf, self.totb
        if self.hg_stop <= 1:
            return
        for d in range(2):
            li = l * 8 + d * 4 + j
            lb = self.lbT[:, 0, li:li + 1]
            omlb = self.lbT[:, 1, li:li + 1]
            self.proj_fm(slot, skey, (1 + d) * 128, F1, ["F8_1"], AF.Sigmoid)
            S.op("dve", "tensor_scalar", reads=["F8_1", "lbT"], writes=["F8_1"], out=F1, in0=F1, scalar1=omlb, scalar2=lb, op0=ALU.mult, op1=ALU.add)
            S.op("dve", "tensor_scalar", reads=["F8_1"], writes=["F8_2"], out=F2, in0=F1, scalar1=-1.0, scalar2=1.0, op0=ALU.mult, op1=ALU.add)
            S.op("act", "activation", reads=["F8_1"], writes=["F8_1"], out=F1, in_=F1, func=AF.Ln)
            S.op("dve", "tensor_tensor_scan", reads=["F8_1", "rmask32"], writes=["F8_3"], out=F3, data0=self.rmask32[:], data1=F1,
                 initial=0.0, op0=ALU.mult, op1=ALU.add)
            if self.hg_stop <= 2:
                continue
            S.op("dve", "tensor_copy", reads=["F8_3"], writes=["totb"], out=totb[:], in_=self.F8[3][:, 31:SEQ:32])
            S.op("act", "activation", reads=["totb"], writes=["Ebuf%d" % d], out=Eb[:, d, :], in_=totb[:], func=AF.Exp)
            tot_bc = totb[:].unsqueeze(2).to_broadcast([128, 64, 32])
            v3 = lambda ap: ap.rearrange("p (c t) -> p c t", t=32)
            if d == 0:
                S.op("dve", "tensor_tensor", reads=["totb", "F8_3"], writes=["F8_1"], out=v3(F1), in0=tot_bc, in1=v3(F3), op=ALU.subtract)
                ysc = 1.0
            else:
                S.op("dve", "tensor_tensor", reads=["F8_1", "F8_3"], writes=["F8_1"], out=F1, in0=F1, in1=F3, op=ALU.subtract)
                S.op("dve", "tensor_tensor", reads=["totb", "F8_1"], writes=["F8_3"], out=v3(F3), in0=v3(F1), in1=tot_bc, op=ALU.add)
                ysc = -1.0
            S.op("act", "activation", reads=["F8_1"], writes=["F8_1"], out=F1, in_=F1, func=AF.Exp, scale=ysc)
            S.op("dve", "tensor_tensor", reads=["F8_1", "F8_2"], writes=self.AK(5), out=A[:, 5, :], in0=F2, in1=F1, op=ALU.mult)
            S.op("act", "activation", reads=["F8_3"], writes=["F8_1"], out=F1, in_=F3, func=AF.Exp, scale=-1.0)
            S.op("dve", "tensor_tensor", reads=["F8_1", "F8_2"], writes=self.AK(4), out=A[:, 4, :], in0=F2, in1=F1, op=ALU.mult)
            S.op("act", "activation", reads=["F8_3"], writes=["F8_3"], out=F3, in_=F3, func=AF.Exp)
            S.op("dve", "scalar_tensor_tensor", reads=["F8_0", "F8_3"], writes=self.AK(d), out=A[:, d, :], in0=F0, scalar=float(128 ** -0.5), in1=F3,
                 op0=ALU.mult, op1=ALU.mult)
            if self.hg_stop <= 3:
                continue
            for g in range(8):
                bi = self.nextbank()
                bank = self.banks[bi]
                for q in range(4):
                    m = g * 4 + q
                    S.op("pe", "matmul", reads=["actT5_%d" % (m // 8), "identb"], writes=["bank%d" % bi], out=bank[0:64, q * 128:(q + 1) * 128],
                         lhsT=A[:, 5, m * 64:(m + 1) * 64], rhs=self.identb[:], start=True, stop=True)
                dst = self.wd[0:64, 4 * d + g // 2, (g % 2) * 512:(g % 2 + 1) * 512]
                self.evac("act" if g % 2 == 0 else "dve", dst, bank[0:64, :], ["bank%d" % bi], ["wd"])
            for g in range(4):
                bi = self.nextbank()
                bank = self.banks[bi]
                for q in range(8):
                    m = g * 8 + q
                    S.op("pe", "matmul", reads=["actT4_%d" % g, "actT%d_%d" % (d, g)], writes=["bank%d" % bi], out=bank[0:64, q * 64:(q + 1) * 64],
                         lhsT=A[:, 4, m * 64:(m + 1) * 64], rhs=A[:, d, m * 64:(m + 1) * 64], start=True, stop=True)
                S.op("dve", "tensor_tensor", reads=["bank%d" % bi, "mask8_%d" % d], writes=["actT%d_%d" % (2 + d, g)],
                     out=A[0:64, 2 + d, g * 512:(g + 1) * 512].rearrange("p (a b) -> p a b", b=64), in0=bank[0:64, :].rearrange("p (a b) -> p a b", b=64),
                     in1=self.mask8[d][:].unsqueeze(1).to_broadcast([64, 8, 64]), op=ALU.mult)
        if self.hg_stop <= 4:
            return
        for d in range(2):
            S.op("pool", "memset", writes=["Sst%d" % d], ap=self.Sst[d][:], constant=0.0)
            S.op("pool", "memset", writes=["Sbf%d_0" % d], ap=self.Sbf[d][0][:], constant=0.0)
        self.rot = [0, 1, 2, 3]
        self.brr = 0
        nstep = [0, 0]
        kvbank = [None, None]
        for m in range(32):
            kvb = [self.nextbank(), self.nextbank()]
            for d in range(2):
                mt = m if d == 0 else 31 - m
                kd = self.wd[0:64, 4 * d + mt // 8, (mt % 8) * 128:(mt % 8 + 1) * 128]
                it = A[0:64, 6 + mt // 16, (mt % 16) * 128:(mt % 16 + 1) * 128]
                itk = "actT%d_%d" % (6 + mt // 16, (mt % 16) // 4)
                for h in range(2):
                    S.op("pe", "matmul", reads=["wd", itk], writes=["bank%d" % kvb[h]], out=self.banks[kvb[h]][:, d * 128:(d + 1) * 128],
                         lhsT=kd[h * 32:(h + 1) * 32, :], rhs=it[h * 32:(h + 1) * 32, :], start=True, stop=True)
            for d in range(2):
                mt = m if d == 0 else 31 - m
                grp = mt // 8
                ob_i = 4 + 2 * d + grp % 2
                ob = self.banks[ob_i]
                sl8 = mt % 8
                it = A[0:64, 6 + mt // 16, (mt % 16) * 128:(mt % 16 + 1) * 128]
                itk = "actT%d_%d" % (6 + mt // 16, (mt % 16) // 4)
                S.op("pe", "matmul", reads=[itk, "actT%d_%d" % (2 + d, mt // 8)], writes=["bank%d" % ob_i], out=ob[:, sl8 * 64:(sl8 + 1) * 64],
                     lhsT=it, rhs=A[0:64, 2 + d, mt * 64:(mt + 1) * 64], start=True, stop=False)
                order = (0, 1) if d == 0 else (1, 0)
                for oi, h in enumerate(order):
                    c = mt * 2 + h
                    n = nstep[d]
                    S.op("pe", "matmul", reads=["Sbf%d_%d" % (d, n % 3), "actT%d_%d" % (d, c // 16)], writes=["bank%d" % ob_i],
                         out=ob[:, sl8 * 64 + h * 32:sl8 * 64 + (h + 1) * 32], lhsT=self.Sbf[d][n % 3][:], rhs=A[:, d, c * 32:(c + 1) * 32],
                         start=False, stop=(oi == 1))
                    S.op("dve", "scalar_tensor_tensor", reads=["Sst%d" % d, "Ebuf%d" % d, "bank%d" % kvb[h]], writes=["Sst%d" % d], out=self.Sst[d][:],
                         in0=self.Sst[d][:], scalar=Eb[:, d, c:c + 1], in1=self.banks[kvb[h]][:, d * 128:(d + 1) * 128], op0=ALU.mult, op1=ALU.add)
                    S.op("act", "activation", reads=["Sst%d" % d], writes=["Sbf%d_%d" % (d, (n + 1) % 3)], out=self.Sbf[d][(n + 1) % 3][:],
                         in_=self.Sst[d][:], func=AF.Identity)
                    nstep[d] += 1
                done = (sl8 == 7) if d == 0 else (sl8 == 0)
                if done:
                    first = (d == 0 and grp < 2) or (d == 1 and grp >= 2)
                    dst = F0[:, grp * 512:(grp + 1) * 512]
                    if first:
                        self.evac("act", dst, ob[:, :], ["bank%d" % ob_i], ["F8_0"])
                    else:
                        S.op("dve", "tensor_tensor", reads=["bank%d" % ob_i, "F8_0"], writes=["F8_0"], out=dst, in0=ob[:, :], in1=dst, op=ALU.add)
        self.rot = list(range(8))
        if self.hg_stop <= 5:
            return
        gslot, gkey = self.load_w_cols(wsrc, [C_HGATE + j * 128])
        self.proj_fm(gslot, gkey, 0, F1, ["F8_1"], AF.Silu)
        self.gated_norm_store(F0, "F8_0", F1, "F8_1", self.pcol("hgnorm", l), dr["o_hg_s"][j], "o_hg_s%d" % j)

    def gated_norm_store(self, O, okey, G, gkey, wcol, dst_dram, dkey):
        S, A = self.S, self.actT
        for c in range(4):
            cs = slice(c * 512, (c + 1) * 512)
            S.op("act", "activation", reads=[okey], writes=["actT4_%d" % c], out=A[:, 4, cs], in_=O[:, cs], func=AF.Square)
            bi = self.nextbank()
            bank = self.banks[bi]
            S.op("pe", "matmul", reads=["onesb", "actT4_%d" % c], writes=["bank%d" % bi], out=bank[:, :], lhsT=self.onesb[:], rhs=A[:, 4, cs],
                 start=True, stop=True)
            rst = self.F8[3][:, (c % 2) * 512:(c % 2 + 1) * 512]
            rk = "F8_3"
            S.op("act", "activation", reads=["bank%d" % bi], writes=[rk], out=rst, in_=bank[:, :], func=AF.Sqrt, bias=EPS, scale=1.0 / 128)
            S.op("dve", "reciprocal", reads=[rk], writes=[rk], out=rst, in_=rst)
            S.op("dve", "scalar_tensor_tensor", reads=[okey, rk, "prm"], writes=[rk], out=rst, in0=O[:, cs], scalar=wcol, in1=rst, op0=ALU.mult, op1=ALU.mult)
            S.op("dve", "tensor_tensor", reads=[rk, gkey], writes=["actT5_%d" % c], out=A[:, 5, cs], in0=rst, in1=G[:, cs], op=ALU.mult)
        S.dma("sp", reads=self.AK(5), writes=[dkey], out=dst_dram, in_=A[:, 5, :])

    def dn_scalars(self, l):
        S, dr, hT, tk, cf = self.S, self.dr, self.hT, self.tk, self.cf
        slot, skey = self.load_w_cols(dr["w_in"][l], [C_BETA], width=16)
        bi = self.nextbank()
        bank = self.banks[bi]
        for tt in range(16):
            for kt in range(KT):
                S.op("pe", "matmul", reads=[skey] + self.hkeys(PAD + tt * 128, PAD + (tt + 1) * 128), writes=["bank%d" % bi],
                     out=bank[:, tt * 16:(tt + 1) * 16], lhsT=hT[:, kt, PAD + tt * 128:PAD + (tt + 1) * 128], rhs=slot[:, kt, 0:16],
                     start=(kt == 0), stop=(kt == KT - 1))
        b3 = bank[:, 0:256].rearrange("p (t c) -> p t c", c=16)
        allk = ["tk"]
        S.op("act", "activation", reads=["bank%d" % bi], writes=allk, out=tk["beta"][:], in_=b3[:, :, 0:8], func=AF.Sigmoid)
        S.op("dve", "tensor_scalar", reads=allk, writes=allk, out=tk["nbeta"][:], in0=tk["beta"][:], scalar1=-1.0, scalar2=None, op0=ALU.mult)
        dtb = self.pcol("dtb", l, 0, 8).unsqueeze(1).to_broadcast([128, 16, 8])
        S.op("dve", "tensor_tensor", reads=["bank%d" % bi, "prm"], writes=allk, out=tk["g"][:], in0=b3[:, :, 8:16], in1=dtb, op=ALU.add)
        S.op("act", "activation", reads=allk, writes=allk, out=tk["g"][:], in_=tk["g"][:], func=AF.Exp)
        S.op("act", "activation", reads=allk, writes=allk, out=tk["g"][:], in_=tk["g"][:], func=AF.Ln, bias=1.0, scale=1.0)
        S.op("act", "activation", reads=["prm"], writes=["nea"], out=self.nea, in_=self.pcol("alog", l, 0, 8), func=AF.Exp)
        S.op("dve", "tensor_scalar", reads=["nea"], writes=["nea"], out=self.nea, in0=self.nea, scalar1=-1.0, scalar2=None, op0=ALU.mult)
        S.op("dve", "tensor_tensor", reads=allk + ["nea"], writes=allk, out=tk["g"][:], in0=tk["g"][:],
             in1=self.nea.unsqueeze(1).to_broadcast([128, 16, 8]), op=ALU.mult)
        g2 = tk["g"][:].rearrange("p t c -> p (t c)")
        bj = self.nextbank()
        bk2 = self.banks[bj]
        S.op("pe", "matmul", reads=allk + ["cf_U2"], writes=["bank%d" % bj], out=bk2[:, 0:128], lhsT=cf["U2"][:], rhs=g2, start=True, stop=True)
        S.op("pe", "matmul", reads=allk + ["cf_L2"], writes=["bank%d" % bj], out=bk2[:, 128:256], lhsT=cf["L2"][:], rhs=g2, start=True, stop=True)
        S.op("pe", "matmul", reads=allk + ["cf_B2"], writes=["bank%d" % bj], out=bk2[:, 256:384], lhsT=cf["B2"][:], rhs=g2, start=True, stop=True)
        v3 = lambda ap: ap.rearrange("p (t c) -> p t c", c=8)
        S.op("dve", "tensor_copy", reads=["bank%d" % bj], writes=allk, out=tk["gc"][:, :, 0:4], in_=v3(bk2[:, 0:128])[:, :, 0:4])
        S.op("dve", "tensor_copy", reads=["bank%d" % bj], writes=allk, out=tk["gc"][:, :, 4:8], in_=v3(bk2[:, 128:256])[:, :, 4:8])
        S.op("dve", "tensor_copy", reads=["bank%d" % bj], writes=allk, out=tk["tot"][:], in_=v3(bk2[:, 256:384]))
        S.op("dve", "tensor_scalar", reads=allk, writes=allk, out=tk["ngc"][:], in0=tk["gc"][:], scalar1=-1.0, scalar2=None, op0=ALU.mult)
        S.op("act", "activation", reads=allk, writes=allk, out=tk["eg"][:], in_=tk["gc"][:], func=AF.Exp)
        S.op("dve", "tensor_tensor", reads=allk, writes=allk, out=tk["ekd"][:], in0=tk["tot"][:], in1=tk["gc"][:], op=ALU.subtract)
        S.op("act", "activation", reads=allk, writes=allk, out=tk["ekd"][:], in_=tk["ekd"][:], func=AF.Exp)
        t2 = tk["tot"][:].rearrange("p t c -> p (t c)")
        bq = self.nextbank()
        bk3 = self.banks[bq]
        S.op("pe", "matmul", reads=allk + ["cf_H0"], writes=["bank%d" % bq], out=bk3[:, 0:128], lhsT=cf["H0"][:], rhs=t2, start=True, stop=True)
        S.op("pe", "matmul", reads=allk + ["cf_H1"], writes=["bank%d" % bq], out=bk3[:, 128:256], lhsT=cf["H1"][:], rhs=t2, start=True, stop=True)
        S.op("act", "activation", reads=["bank%d" % bq], writes=["glb"], out=self.F8[3][:, 1024:1280], in_=bk3[:, 0:256], func=AF.Exp)

    def dn_head(self, l, j):
        S, dr, A, hT, tk, cf, mt, mtf = self.S, self.dr, self.actT, self.hT, self.tk, self.cf, self.mt, self.mtf
        F0, F1 = self.F8[0], self.F8[1]
        wsrc = dr["w_in"][l]
        slot, skey = self.load_w_cols(wsrc, [j * 128, 512 + j * 128, 1024 + j * 128])
        for xi in range(3):
            for (a, b) in pieces(SP):
                bi = self.nextbank()
                self.proj(slot, skey, xi * 128, a, b, bi)
                self.evac("act", F0[:, a:b], self.banks[bi][:, 0:b - a], ["bank%d" % bi], ["F8_0"])
            tile = xi * 4 + j
            cv = F1[:, 0:SEQ]
            S.op("dve", "tensor_scalar", reads=["F8_0", "prm"], writes=["F8_1"], out=cv, in0=F0[:, 0:SEQ], scalar1=self.pcol("dnconv", l, tile * 5),
                 scalar2=None, op0=ALU.mult)
            for k in range(1, 5):
                S.op("dve", "scalar_tensor_tensor", reads=["F8_0", "F8_1", "prm"], writes=["F8_1"], out=cv, in0=F0[:, k:k + SEQ],
                     scalar=self.pcol("dnconv", l, tile * 5 + k), in1=cv, op0=ALU.mult, op1=ALU.add)
            S.op("act", "activation", reads=["F8_1"], writes=["F8_1"], out=cv, in_=cv, func=AF.Silu)
            if xi == 2:
                S.op("dve", "tensor_copy", reads=["F8_1"], writes=self.AK(2), out=A[:, 2, :], in_=cv)
                continue
            for c in range(4):
                cs = slice(c * 512, (c + 1) * 512)
                S.op("act", "activation", reads=["F8_1"], writes=["actT7_%d" % c], out=A[:, 7, cs], in_=cv[:, cs], func=AF.Square)
                bi = self.nextbank()
                bank = self.banks[bi]
                S.op("pe", "matmul", reads=["onesb", "actT7_%d" % c], writes=["bank%d" % bi], out=bank[:, :], lhsT=self.onesb[:], rhs=A[:, 7, cs],
                     start=True, stop=True)
                rst = self.F8[3][:, (c % 2) * 512:(c % 2 + 1) * 512]
                S.op("act", "activation", reads=["bank%d" % bi], writes=["F8_3"], out=rst, in_=bank[:, :], func=AF.Sqrt, bias=EPS, scale=1.0)
                S.op("dve", "reciprocal", reads=["F8_3"], writes=["F8_3"], out=rst, in_=rst)
                S.op("dve", "scalar_tensor_tensor", reads=["F8_1", "F8_3"], writes=["actT%d_%d" % (xi, c)], out=A[:, xi, cs], in0=cv[:, cs],
                     scalar=(float(128 ** -0.5) if xi == 0 else 1.0), in1=rst, op0=ALU.mult, op1=ALU.mult)
        for (srcs, dsts) in ((1, 3), (2, 4)):
            for g in range(4):
                bi = self.nextbank()
                bank = self.banks[bi]
                for q in range(4):
                    tt = g * 4 + q
                    S.op("pe", "matmul", reads=["actT%d_%d" % (srcs, g), "identb"], writes=["bank%d" % bi], out=bank[:, q * 128:(q + 1) * 128],
                         lhsT=A[:, srcs, tt * 128:(tt + 1) * 128], rhs=self.identb[:], start=True, stop=True)
                self.evac("act" if g % 2 == 0 else "dve", A[:, dsts, g * 512:(g + 1) * 512], bank[:, :], ["bank%d" % bi], ["actT%d_%d" % (dsts, g)])
        k3 = A[:, 3, :].rearrange("p (t d) -> p t d", d=128)
        for d in range(2):
            col = d * 4 + j
            S.op("dve", "tensor_tensor", reads=self.AK(3) + ["tk"], writes=self.AK(5 + d), out=A[:, 5 + d, :].rearrange("p (t d) -> p t d", d=128),
                 in0=k3, in1=tk["ekd"][:, :, col:col + 1].to_broadcast([128, 16, 128]), op=ALU.mult)
        for d in range(2):
            col = d * 4 + j
            dg = A[:, 7, :].rearrange("p (t d) -> p t d", d=128)
            S.op("dve", "tensor_tensor", reads=["identf", "tk"], writes=self.AK(7), out=dg, in0=self.identf[:].unsqueeze(1).to_broadcast([128, 16, 128]),
                 in1=tk["eg"][:, :, col:col + 1].to_broadcast([128, 16, 128]), op=ALU.mult)
            for c in range(4):
                cs = slice(c * 512, (c + 1) * 512)
                bi = self.nextbank()
                bank = self.banks[bi]
                S.op("pe", "matmul", reads=["onesb", "actT7_%d" % c], writes=["bank%d" % bi], out=bank[:, :], lhsT=self.onesb[:], rhs=A[:, 7, cs],
                     start=True, stop=True)
                S.op("dve", "tensor_tensor", reads=["bank%d" % bi, "actT0_%d" % c], writes=["wd"], out=self.wd[:, 2 * d + c // 2, (c % 2) * 512:(c % 2 + 1) * 512],
                     in0=bank[:, :], in1=A[:, 0, cs], op=ALU.mult)
        self.barrier()
        qkv = [self.F8[2][:, 0:1024].bitcast(BF16), self.F8[3][:, 0:1024].bitcast(BF16)]
        qkk = ["F8_2", "F8_3"]
        I_b = self.identb
        self.rot = list(range(7))
        self.brr = 0
        for tt in range(16):
            ts = slice(tt * 128, (tt + 1) * 128)
            bkk = 7
            S.op("pe", "matmul", reads=["actT1_%d" % (tt // 4)], writes=["bank%d" % bkk], out=self.banks[bkk][:, 0:128], lhsT=A[:, 1, ts], rhs=A[:, 1, ts],
                 start=True, stop=True)
            S.op("pe", "matmul", reads=["actT1_%d" % (tt // 4), "actT0_%d" % (tt // 4)], writes=["bank%d" % bkk], out=self.banks[bkk][:, 128:256],
                 lhsT=A[:, 1, ts], rhs=A[:, 0, ts], start=True, stop=True)
            KK = self.banks[bkk][:, 0:128]
            KQ = self.banks[bkk][:, 128:256]
            for d in range(2):
                col = d * 4 + j
                sc = lambda nm: tk[nm][:, tt, col:col + 1]
                dn = "fb"[d]
                S.op("dve", "tensor_scalar", reads=["identf", "tk"], writes=["mtf_dg"], out=mtf["dg"][:], in0=self.identf[:], scalar1=sc("gc"), scalar2=None, op0=ALU.mult)
                bp = self.nextbank()
                P1 = self.banks[bp][:, 0:128]
                P2 = self.banks[bp][:, 128:256]
                for (Pm, M) in ((P1, "M1" + dn), (P2, "M2" + dn)):
                    S.op("pe", "matmul", reads=["cf_ones", "mtf_dg"], writes=["bank%d" % bp], out=Pm, lhsT=cf["ones"][:], rhs=mtf["dg"][:], start=True, stop=False)
                    S.op("pe", "matmul", reads=["identf", "cf_" + M], writes=["bank%d" % bp], out=Pm, lhsT=self.identf[:], rhs=cf[M][:], start=False, stop=True)
                S.op("act", "activation", reads=["bank%d" % bp, "tk"], writes=["mtf_EA"], out=mtf["EA"][:], in_=P1, func=AF.Exp, scale=-1.0, bias=sc("gc"))
                S.op("act", "activation", reads=["bank%d" % bp, "tk"], writes=["mtf_EQ"], out=mtf["EQ"][:], in_=P2, func=AF.Exp, scale=1.0, bias=sc("ngc"))
                S.op("dve", "tensor_tensor", reads=["bank%d" % bkk, "mtf_EQ"], writes=[qkk[d]], out=qkv[d][:, ts], in0=KQ, in1=mtf["EQ"][:], op=ALU.mult)
                Bd, Bo = mt["B"][:, 0:128], mt["B"][:, 128:256]
                S.op("dve", "scalar_tensor_tensor", reads=["bank%d" % bkk, "mtf_EA", "tk"], writes=["mt_B"], out=Bo, in0=KK, scalar=sc("nbeta"), in1=mtf["EA"][:],
                     op0=ALU.mult, op1=ALU.mult)
                S.op("dve", "tensor_tensor", reads=["mt_B", "mask16"], writes=["mt_B"], out=Bd, in0=Bo, in1=self.mask16[:], op=ALU.mult)
                S.op("dve", "tensor_tensor", reads=["mt_B"], writes=["mt_B"], out=Bo, in0=Bo, in1=Bd, op=ALU.subtract)
                bt = self.nextbank()
                for h in range(2):
                    S.op("pe", "matmul", reads=["mt_B", "identb"], writes=["bank%d" % bt], out=self.banks[bt][:, h * 128:(h + 1) * 128],
                         lhsT=mt["B"][:, h * 128:(h + 1) * 128], rhs=I_b[:], start=True, stop=True)
                self.evac("act", mt["BT"][:], self.banks[bt][:, 0:256], ["bank%d" % bt], ["mt_BT"])
                BdT, BoT = mt["BT"][:, 0:128], mt["BT"][:, 128:256]
                SQ, SQT = mt["SQ"], mt["SQT"]
                S.op("dve", "tensor_tensor", reads=["mt_B", "identb"], writes=["mt_SQ"], out=SQ[:, 0:128], in0=Bd, in1=I_b[:], op=ALU.add)
                S.op("dve", "tensor_tensor", reads=["mt_BT", "identb"], writes=["mt_SQT"], out=SQT[:, 0:128], in0=BdT, in1=I_b[:], op=ALU.add)
                b0 = self.nextbank()
                S.op("pe", "matmul", reads=["mt_B", "mt_BT"], writes=["bank%d" % b0], out=self.banks[b0][:, 0:128], lhsT=BdT, rhs=Bd, start=True, stop=True)
                S.op("pe", "matmul", reads=["mt_B", "mt_BT"], writes=["bank%d" % b0], out=self.banks[b0][:, 128:256], lhsT=Bd, rhs=BdT, start=True, stop=True)
                self.evac("act", SQ[:, 128:256], self.banks[b0][:, 0:128], ["bank%d" % b0], ["mt_SQ"])
                self.evac("act", SQT[:, 128:256], self.banks[b0][:, 128:256], ["bank%d" % b0], ["mt_SQT"])
                for lvl in range(3):
                    last = (lvl == 2)
                    n = 128 if last else 256
                    b1 = self.nextbank()
                    b2 = self.nextbank()
                    S.op("pe", "matmul", reads=["mt_SQ", "mt_SQT"], writes=["bank%d" % b1], out=self.banks[b1][:, 0:n], lhsT=SQT[:, 128:256], rhs=SQ[:, 0:n], start=True, stop=True)
                    S.op("pe", "matmul", reads=["mt_SQ", "mt_SQT"], writes=["bank%d" % b2], out=self.banks[b2][:, 0:n], lhsT=SQ[:, 128:256], rhs=SQT[:, 0:n], start=True, stop=True)
                    S.op("dve", "tensor_tensor", reads=["bank%d" % b1, "mt_SQ"], writes=["mt_SQ"], out=SQ[:, 0:128], in0=self.banks[b1][:, 0:128], in1=SQ[:, 0:128], op=ALU.add)
                    S.op("dve", "tensor_tensor", reads=["bank%d" % b2, "mt_SQT"], writes=["mt_SQT"], out=SQT[:, 0:128], in0=self.banks[b2][:, 0:128], in1=SQT[:, 0:128], op=ALU.add)
                    if not last:
                        self.evac("act", SQ[:, 128:256], self.banks[b1][:, 128:256], ["bank%d" % b1], ["mt_SQ"])
                        self.evac("act", SQT[:, 128:256], self.banks[b2][:, 128:256], ["bank%d" % b2], ["mt_SQT"])
                Td, TdT = SQ[:, 0:128], SQT[:, 0:128]
                bn = self.nextbank()
                S.op("pe", "matmul", reads=["mt_BT", "mt_SQ"], writes=["bank%d" % bn], out=self.banks[bn][:, 0:128], lhsT=BoT, rhs=Td, start=True, stop=True)
                self.evac("act", mt["N"][:, 0:128], self.banks[bn][:, 0:128], ["bank%d" % bn], ["mt_N"])
                Nn = mt["N"][:, 0:128]
                Y0 = TdT
                S.op("dve", "tensor_scalar", reads=["mt_SQT", "tk"], writes=["mt_Y0b"], out=mt["Y0b"][:, 0:128], in0=Y0, scalar1=sc("beta"), scalar2=None, op0=ALU.mult)
                Z = mt["Z"]
                prev, prevk = Y0, "mt_SQT"
                for it in range(3):
                    bz = self.nextbank()
                    S.op("pe", "matmul", reads=["mt_N", prevk], writes=["bank%d" % bz], out=self.banks[bz][:, 0:128], lhsT=Nn, rhs=prev, start=True, stop=True)
                    if it < 2:
                        zo = Z[:, (it % 2) * 128:(it % 2 + 1) * 128]
                        S.op("dve", "tensor_tensor", reads=["bank%d" % bz, "mt_SQT"], writes=["mt_Z"], out=zo, in0=self.banks[bz][:, 0:128], in1=Y0, op=ALU.add)
                        prev, prevk = zo, "mt_Z"
                    else:
                        TTb = self.wd[:, 4 + 2 * d + tt // 8, (tt % 8) * 128:(tt % 8 + 1) * 128]
                        S.op("dve", "scalar_tensor_tensor", reads=["bank%d" % bz, "mt_Y0b", "tk"], writes=["wd"], out=TTb, in0=self.banks[bz][:, 0:128], scalar=sc("beta"),
                             in1=mt["Y0b"][:, 0:128], op0=ALU.mult, op1=ALU.add)
                TTbg = mt["Z"][:, 0:128]
                S.op("dve", "tensor_scalar", reads=["wd", "tk"], writes=["mt_Z"], out=TTbg, in0=TTb, scalar1=sc("eg"), scalar2=None, op0=ALU.mult)
                bw = self.nextbank()
                S.op("pe", "matmul", reads=["actT3_%d" % (tt // 4), "mt_Z"], writes=["bank%d" % bw], out=self.banks[bw][:, 0:128], lhsT=A[:, 3, ts], rhs=TTbg, start=True, stop=True)
                S.op("act", "activation", reads=["bank%d" % bw], writes=["actT%d_%d" % ((2, 7)[d], tt // 4)], out=A[:, (2, 7)[d], ts], in_=self.banks[bw][:, 0:128], func=AF.Identity, scale=-1.0)
        for d in range(2):
            S.op("pool", "memset", writes=["Sst%d" % d], ap=self.Sst[d][:], constant=0.0)
            S.op("pool", "memset", writes=["Sbf%d_0" % d], ap=self.Sbf[d][0][:], constant=0.0)
        self.rot = [0, 1, 2, 3]
        self.brr = 0
        oT = self.F8[0][:, 0:SEQ]
        vnb = [self.mt["Y0"], self.mt["Y0b"]]
        for h in range(2):
            S.op("pool", "memset", writes=["vn0", "vn1", "mt_Y0b", "mt_Y0"], ap=vnb[h][:, :], constant=0.0)
        for n in range(32):
            for d in range(2):
                c = n if d == 0 else 31 - n
                tt, h = c // 2, c % 2
                ps_ = slice(h * 64, (h + 1) * 64)
                col = d * 4 + j
                grp = c // 8
                ob_i = 4 + 2 * d + grp % 2
                ob = self.banks[ob_i]
                sl8 = c % 8
                Sb = self.Sbf[d][n % 3]
                Sbk = "Sbf%d_%d" % (d, n % 3)
                cs = slice(c * 64, (c + 1) * 64)
                TTb = self.wd[:, 4 + 2 * d + tt // 8, (tt % 8) * 128:(tt % 8 + 1) * 128]
                nw = (2, 7)[d]
                bv = self.rot[h * 2 + (n % 2)]
                pv = self.banks[bv][ps_, d * 128:(d + 1) * 128]
                S.op("pe", "matmul", reads=["wd", "actT4_%d" % (tt // 4)], writes=["bank%d" % bv], out=pv, lhsT=TTb[:, h * 64:(h + 1) * 64], rhs=A[:, 4, tt * 128:(tt + 1) * 128],
                     start=True, stop=False)
                S.op("pe", "matmul", reads=["actT%d_%d" % (nw, c // 8), Sbk], writes=["bank%d" % bv], out=pv, lhsT=A[:, nw, cs], rhs=Sb[:], start=False, stop=True)
                vfull = vnb[h][:, d * 128:(d + 1) * 128]
                S.op("act", "activation", reads=["bank%d" % bv], writes=["vn%d" % d], out=vnb[h][ps_, d * 128:(d + 1) * 128], in_=pv, func=AF.Identity)
                psS = self.banks[bv][:, 256 + d * 128:256 + (d + 1) * 128]
                S.op("pe", "matmul", reads=["actT%d_%d" % (5 + d, tt // 4), "vn%d" % d], writes=["bank%d" % bv], out=psS, lhsT=A[:, 5 + d, tt * 128:(tt + 1) * 128], rhs=vfull,
                     start=True, stop=True)
                S.op("dve", "scalar_tensor_tensor", reads=["Sst%d" % d, "glb", "bank%d" % bv], writes=["Sst%d" % d], out=self.Sst[d][:], in0=self.Sst[d][:],
                     scalar=self.glb[:, h, tt, col:col + 1], in1=psS, op0=ALU.mult, op1=ALU.add)
                S.op("act", "activation", reads=["Sst%d" % d], writes=["Sbf%d_%d" % (d, (n + 1) % 3)], out=self.Sbf[d][(n + 1) % 3][:], in_=self.Sst[d][:], func=AF.Identity)
                oslot = ob[:, sl8 * 64:(sl8 + 1) * 64]
                S.op("pe", "matmul", reads=[Sbk, "wd"], writes=["bank%d" % ob_i], out=oslot, lhsT=Sb[:], rhs=self.wd[:, 2 * d + c // 16, (c % 16) * 64:(c % 16 + 1) * 64],
                     start=True, stop=False)
                S.op("pe", "matmul", reads=["vn%d" % d, qkk[d]], writes=["bank%d" % ob_i], out=oslot, lhsT=vfull, rhs=qkv[d][:, tt * 128 + h * 64:tt * 128 + (h + 1) * 64],
                     start=False, stop=True)
                done = (sl8 == 7) if d == 0 else (sl8 == 0)
                if done:
                    first = (d == 0 and grp < 2) or (d == 1 and grp >= 2)
                    dst = oT[:, grp * 512:(grp + 1) * 512]
                    if first:
                        self.evac("act", dst, ob[:, :], ["bank%d" % ob_i], ["F8_0"])
                    else:
                        S.op("dve", "tensor_tensor", reads=["bank%d" % ob_i, "F8_0"], writes=["F8_0"], out=dst, in0=ob[:, :], in1=dst, op=ALU.add)
        self.rot = list(range(8))
        if self.dbg_head is not None:
            raise StopIteration
        self.barrier()
        gslot, gkey = self.load_w_cols(wsrc, [C_Z + j * 128])
        self.proj_fm(gslot, gkey, 0, F1[:, 0:SEQ], ["F8_1"], AF.Silu)
        self.gated_norm_store(oT, "F8_0", F1[:, 0:SEQ], "F8_1", self.pcol("dnnorm", l), dr["o_dn_s"][j], "o_dn_s%d" % j)

    def merge(self, l):
        S, dr, A, xT = self.S, self.dr, self.actT, self.xT
        F0, F1 = self.F8[0][:, 0:SEQ], self.F8[1][:, 0:SEQ]
        for hd in range(4):
            if self.do_dn:
                S.dma("sp", reads=["o_dn_s%d" % hd], writes=self.AK(hd), out=A[:, hd, :], in_=dr["o_dn_s"][hd])
            else:
                S.op("pool", "memset", writes=self.AK(hd), ap=A[:, hd, :], constant=0.0)
            if self.do_hg:
                S.dma("sp", reads=["o_hg_s%d" % hd], writes=self.AK(4 + hd), out=A[:, 4 + hd, :], in_=dr["o_hg_s"][hd])
            else:
                S.op("pool", "memset", writes=self.AK(4 + hd), ap=A[:, 4 + hd, :], constant=0.0)
        mg = self.wd
        for grp in range(2):
            for q in range(4):
                ct = grp * 4 + q
                mdst = self.wd[:, 2 * q:2 * q + 2, :]
                for bidx, (gcol, wname, s0) in enumerate(((C_GDN, "w_bdn", 0), (C_GHG, "w_bhg", 4))):
                    gs, gk = self.load_w_cols(dr["w_in"][l], [gcol + ct * 128])
                    self.proj_fm(gs, gk, 0, F0, ["F8_0"], AF.Sigmoid)
                    i = self.wrr
                    self.wrr = (self.wrr + 1) % len(self.wslot)
                    bs = self.wslot[i]
                    S.dma("pool", writes=[self.WK(i)], out=bs[:, 0:4, 0:128],
                          in_=dr[wname][l, :, ct * 128:(ct + 1) * 128].rearrange("(kt p) n -> p kt n", p=128))
                    for c in range(4):
                        bi = self.nextbank()
                        bank = self.banks[bi]
                        for hd in range(4):
                            S.op("pe", "matmul", reads=[self.WK(i), "actT%d_%d" % (s0 + hd, c)], writes=["bank%d" % bi], out=bank[:, :],
                                 lhsT=bs[:, hd, 0:128], rhs=A[:, s0 + hd, c * 512:(c + 1) * 512], start=(hd == 0), stop=(hd == 3))
                        cs = slice(c * 512, (c + 1) * 512)
                        if bidx == 0:
                            S.op("dve", "tensor_tensor", reads=["bank%d" % bi, "F8_0"], writes=["F8_1"], out=F1[:, cs], in0=bank[:, :], in1=F0[:, cs], op=ALU.mult)
                        else:
                            S.op("dve", "tensor_tensor", reads=["bank%d" % bi, "F8_0"], writes=["F8_0"], out=F0[:, cs], in0=bank[:, :], in1=F0[:, cs], op=ALU.mult)
                            S.op("pool", "tensor_tensor", reads=["F8_0", "F8_1"], writes=["wd"], out=mdst[:, c // 2, (c % 2) * 512:(c % 2 + 1) * 512],
                                 in0=F0[:, cs], in1=F1[:, cs], op=ALU.add)
            i = self.wrr
            self.wrr = (self.wrr + 1) % len(self.wslot)
            ws = self.wslot[i]
            wv = ws[:, :, :].rearrange("p a b -> p (a b)")
            S.dma("pool", writes=[self.WK(i)], out=ws[:, :, :].rearrange("p (k two) n -> p k (two n)", two=2),
                  in_=dr["w_out"][l, grp * 512:(grp + 1) * 512, :].rearrange("(kt p) n -> p kt n", p=128))
            for dm in range(KT):
                for c in range(4):
                    bi = self.nextbank()
                    bank = self.banks[bi]
                    for q in range(4):
                        S.op("pe", "matmul", reads=[self.WK(i), "wd"], writes=["bank%d" % bi], out=bank[:, :],
                             lhsT=wv[:, q * 1024 + dm * 128:q * 1024 + (dm + 1) * 128],
                             rhs=self.wd[:, 2 * q + c // 2, (c % 2) * 512:(c % 2 + 1) * 512], start=(q == 0), stop=(q == 3))
                    S.op("dve", "tensor_tensor", reads=["bank%d" % bi, "xT%d_%d" % (dm, c)], writes=["xT%d_%d" % (dm, c)],
                         out=xT[:, dm, c * 512:(c + 1) * 512], in0=bank[:, :], in1=xT[:, dm, c * 512:(c + 1) * 512], op=ALU.add)

    def final_store(self, s):
        S, dr, xT, identf = self.S, self.dr, self.xT, self.identf
        for c in range(4):
            rst = self.F8[3][:, (c % 2) * 512:(c % 2 + 1) * 512]
            sqv = self.actT[:, 0:8, c * 512:(c + 1) * 512]
            self.rstd_chunk(c, rst, "F8_3", sqv, "sq%d_" % c)
            for kt in range(KT):
                ybuf = self.F8[kt // 4]
                S.op("dve", "scalar_tensor_tensor", reads=["xT%d_%d" % (kt, c), "F8_3", "prm"], writes=["F8_%d" % (kt // 4)],
                     out=ybuf[:, (kt % 4) * 512:(kt % 4 + 1) * 512], in0=xT[:, kt, c * 512:(c + 1) * 512],
                     scalar=self.pcol("finaln", None, kt), in1=rst, op0=ALU.mult, op1=ALU.mult)
            for t4 in range(4):
                tt = c * 4 + t4
                otile = self.F8[2]
                okey = "F8_2"
                for h2 in range(2):
                    bi = self.nextbank()
                    bank = self.banks[bi]
                    for j in range(4):
                        kt = h2 * 4 + j
                        ybuf = self.F8[kt // 4]
                        S.op("pe", "transpose", reads=["F8_%d" % (kt // 4), "identf"], writes=["bank%d" % bi],
                             out=bank[:, j * 128:(j + 1) * 128],
                             in_=ybuf[:, (kt % 4) * 512 + t4 * 128:(kt % 4) * 512 + (t4 + 1) * 128], identity=identf[:])
                    self.evac("act", otile[:, h2 * 512:(h2 + 1) * 512], bank[:, :], ["bank%d" % bi], [okey])
                S.dma("sp", reads=[okey], writes=["out"], out=dr["out"][s, tt * 128:(tt + 1) * 128, :], in_=otile[:, 0:D])


_CACHE = {}


def get_nc(nseq, **kw):
    key = (nseq, tuple(sorted(kw.items())))
    if key not in _CACHE:
        _CACHE[key] = Builder(nseq, **kw).build()
    return _CACHE[key]


def kernel(x, mix_norm, w_in, dn_conv, dn_a_log, dn_dt_bias, dn_norm, hg_lb_logits, hg_norm, w_branch_dn, w_branch_hg,
           w_out, ffn_norm, w_up, ffn_conv, ffn_conv_bias, w_down, final_norm):
    f = lambda a: np.ascontiguousarray(np.asarray(a, dtype=np.float32))
    x = f(x)
    P = pack_params(f(mix_norm), f(dn_conv), f(dn_a_log), f(dn_dt_bias), f(dn_norm), f(hg_lb_logits), f(hg_norm),
                    f(ffn_norm), f(ffn_conv), f(ffn_conv_bias), f(final_norm))
    nseq = x.shape[0] // NCORES
    nc = get_nc(nseq)
    shared = {"params": P, "w_in": f(w_in), "w_branch_dn": f(w_branch_dn), "w_branch_hg": f(w_branch_hg),
              "w_out": f(w_out), "w_up": f(w_up), "w_down": f(w_down)}
    in_maps = []
    for c in range(NCORES):
        m = dict(shared)
        m["x"] = x[c * nseq:(c + 1) * nseq]
        in_maps.append(m)
    res = run_bass_kernel_spmd(nc, in_maps, core_ids=list(range(NCORES)))
    return np.concatenate([r["out"] for r in res.results], axis=0)
```
